# Optimizing a Trainium2 kernel written in Bass

```python
import math
import jax, jax.numpy as jnp
from jax import lax
import numpy as np

D_MODEL = 1024
BATCH = 8
SEQ = 2048
DEPTH = 1
DEC_BATCH = 32
DEC_SEQ = 8
PAST_LEN = 8192
PAGE_SIZE = 128

HEAD_DIM = 64
GROUPS = ((128, 1), (512, 4), (2048, 16))
N_GROUPS = 3
HEADS_PER_GROUP = 4
N_ATTN_HEADS = N_GROUPS * HEADS_PER_GROUP
ATTN_WIDTH = N_ATTN_HEADS * HEAD_DIM
ATTN_OUT_WIDTH = HEADS_PER_GROUP * HEAD_DIM
CONV_DIM = D_MODEL // 2
CONV_WIDTH = 31
N_BUCKETS = 32
MAX_DISTANCE = 2048
D_FF = ((8 * D_MODEL // 3 + 127) // 128) * 128
N_MEM = 256
X_HEADS = 4
X_HEAD_DIM = D_MODEL // X_HEADS
EPS = 1e-6
NEG_INF = -1e30
IN_WIDTH = 3 * ATTN_WIDTH + 2 * CONV_DIM + 2 * D_MODEL
SPLIT_POINTS = (ATTN_WIDTH, 2 * ATTN_WIDTH, 3 * ATTN_WIDTH,
                3 * ATTN_WIDTH + CONV_DIM, 3 * ATTN_WIDTH + 2 * CONV_DIM,
                3 * ATTN_WIDTH + 2 * CONV_DIM + D_MODEL)

kernel_name = 'hybrid_dilated_conformer_decode_step'


def rmsnorm(x, g):
    xf = x.astype(jnp.float32)
    y = xf * lax.rsqrt(jnp.mean(xf * xf, axis=-1, keepdims=True) + EPS)
    return (y * g.astype(jnp.float32)).astype(x.dtype)


def layernorm(x, g, b):
    xf = x.astype(jnp.float32)
    mu = jnp.mean(xf, axis=-1, keepdims=True)
    var = jnp.mean(jnp.square(xf - mu), axis=-1, keepdims=True)
    y = (xf - mu) * lax.rsqrt(var + EPS)
    return (y * g.astype(jnp.float32) + b.astype(jnp.float32)).astype(x.dtype)


def swiglu(x, w_gate, w_up, w_down):
    return (jax.nn.silu(x @ w_gate) * (x @ w_up)) @ w_down


def rel_bucket(dist):
    n = jnp.maximum(dist, 0)
    max_exact = N_BUCKETS // 2
    nf = jnp.maximum(n, 1).astype(jnp.float32)
    large = max_exact + (jnp.log(nf / max_exact) / math.log(MAX_DISTANCE / max_exact)
                         * (N_BUCKETS - max_exact)).astype(jnp.int32)
    return jnp.where(n < max_exact, n, jnp.minimum(large, N_BUCKETS - 1))


def dilated_group_band(q, k, v, bias, window, dil):
    B, S, H, Dh = q.shape
    span = window // dil
    L = S // dil
    n_sub = B * dil
    blk = span
    nb = -(-L // blk)
    Lp = nb * blk

    def to_blocks(a):
        a = a.reshape(B, L, dil, H, Dh).transpose(0, 2, 1, 3, 4).reshape(n_sub, L, H, Dh)
        a = jnp.pad(a, ((0, 0), (0, Lp - L), (0, 0), (0, 0)))
        return a.reshape(n_sub, nb, blk, H, Dh)

    def with_prev(a):
        prev = jnp.pad(a, ((0, 0), (1, 0), (0, 0), (0, 0), (0, 0)))[:, :-1]
        return jnp.concatenate([prev, a], axis=2)

    qb = to_blocks(q)
    kk = with_prev(to_blocks(k))
    vv = with_prev(to_blocks(v))
    kj = jnp.arange(2 * blk)[None, :]
    delta = (jnp.arange(blk)[:, None] + blk) - kj
    in_band = (delta >= 0) & (delta <= span)
    key_idx = jnp.arange(nb)[:, None] * blk - blk + kj
    mask = in_band[None] & (key_idx[:, None, :] >= 0)
    bias_qk = jnp.transpose(bias[rel_bucket(delta * dil)], (2, 0, 1)).astype(jnp.float32)
    s = jnp.einsum('nbqhd,nbkhd->nbhqk', qb, kk).astype(jnp.float32) * (HEAD_DIM ** -0.5)
    s = jnp.where(mask[None, :, None], s + bias_qk[None, None], NEG_INF)
    lse = jax.nn.logsumexp(s, axis=-1)
    p = jnp.exp(s - lse[..., None]).astype(v.dtype)
    o = jnp.einsum('nbhqk,nbkhd->nbqhd', p, vv)
    o = o.reshape(n_sub, Lp, H, Dh)[:, :L]
    lse = lse.transpose(0, 1, 3, 2).reshape(n_sub, Lp, H)[:, :L]

    def from_sub(a):
        tail = a.shape[2:]
        return a.reshape((B, dil, L) + tail).swapaxes(1, 2).reshape((B, S) + tail)

    return from_sub(o), from_sub(lse)


def dilated_group_gather(q, k_new, v_new, k_buf, v_buf, bias, window, dil):
    T = q.shape[1]
    Lb = k_buf.shape[1]
    kc = jnp.concatenate([k_buf.astype(k_new.dtype), k_new], axis=1)
    vc = jnp.concatenate([v_buf.astype(v_new.dtype), v_new], axis=1)
    taps = jnp.arange(window // dil + 1)
    idx = (Lb + jnp.arange(T))[:, None] - taps[None, :] * dil
    valid = idx >= 0
    idx_c = jnp.maximum(idx, 0)
    kg = kc[:, idx_c]
    vg = vc[:, idx_c]
    bias_m = jnp.transpose(bias[rel_bucket(taps * dil)], (1, 0)).astype(jnp.float32)
    s = jnp.einsum('bthd,btmhd->bhtm', q, kg).astype(jnp.float32) * (HEAD_DIM ** -0.5)
    s = jnp.where(valid[None, None], s + bias_m[None, :, None, :], NEG_INF)
    lse = jax.nn.logsumexp(s, axis=-1)
    p = jnp.exp(s - lse[..., None]).astype(vg.dtype)
    o = jnp.einsum('bhtm,btmhd->bthd', p, vg)
    return o, lse.transpose(0, 2, 1), kc[:, -Lb:], vc[:, -Lb:]


def merge_groups(outs, lses, dtype):
    w = jax.nn.softmax(jnp.stack(lses, axis=0).astype(jnp.float32), axis=0)
    o = jnp.einsum('gbth,gbthd->bthd', w, jnp.stack(outs, axis=0).astype(jnp.float32))
    B, T = o.shape[:2]
    return o.reshape(B, T, ATTN_OUT_WIDTH).astype(dtype)


def dilated_attention_prompt(q, k, v, rel_bias):
    S = q.shape[1]
    outs, lses, bk, bv = [], [], [], []
    for g, (window, dil) in enumerate(GROUPS):
        bias_g = rel_bias[:, g * HEADS_PER_GROUP:(g + 1) * HEADS_PER_GROUP]
        o, lse = dilated_group_band(q[:, :, g], k[:, :, g], v[:, :, g], bias_g, window, dil)
        outs.append(o)
        lses.append(lse)
        keep = min(window, S)
        bk.append(k[:, S - keep:, g])
        bv.append(v[:, S - keep:, g])
    return merge_groups(outs, lses, q.dtype), bk, bv


def dilated_attention_sample(q, k, v, k_bufs, v_bufs, rel_bias):
    outs, lses, bk, bv = [], [], [], []
    for g, (window, dil) in enumerate(GROUPS):
        bias_g = rel_bias[:, g * HEADS_PER_GROUP:(g + 1) * HEADS_PER_GROUP]
        o, lse, nk, nv = dilated_group_gather(q[:, :, g], k[:, :, g], v[:, :, g],
                                              k_bufs[g], v_bufs[g], bias_g, window, dil)
        outs.append(o)
        lses.append(lse)
        bk.append(nk)
        bv.append(nv)
    return merge_groups(outs, lses, q.dtype), bk, bv


def conformer_conv(u, buf, p):
    C = u.shape[-1]
    ext = jnp.concatenate([buf.astype(u.dtype), u], axis=1)
    y = lax.conv_general_dilated(ext, p['conv_dw_w'][:, None, :].astype(u.dtype),
                                 window_strides=(1,), padding='VALID',
                                 dimension_numbers=('NWC', 'WIO', 'NWC'),
                                 feature_group_count=C)
    y = jax.nn.silu(layernorm(y + p['conv_dw_b'], p['conv_ln_g'], p['conv_ln_b']))
    return y @ p['w_conv_proj'], ext[:, -(CONV_WIDTH - 1):]


def memory_kv(mem, p):
    B = mem.shape[0]
    m = rmsnorm(mem, p['mem_norm']) @ p['w_xkv']
    mk, mv = jnp.split(m, 2, axis=-1)
    return (mk.reshape(B, N_MEM, X_HEADS, X_HEAD_DIM), mv.reshape(B, N_MEM, X_HEADS, X_HEAD_DIM))


def cross_attention(h, mk, mv, w_xq, w_xo):
    B, T, _ = h.shape
    q = (h @ w_xq).reshape(B, T, X_HEADS, X_HEAD_DIM)
    s = jnp.einsum('bthd,bmhd->bhtm', q, mk.astype(q.dtype)).astype(jnp.float32) * (X_HEAD_DIM ** -0.5)
    pr = jax.nn.softmax(s, axis=-1).astype(q.dtype)
    o = jnp.einsum('bhtm,bmhd->bthd', pr, mv.astype(q.dtype)).reshape(B, T, D_MODEL)
    return o @ w_xo


def decoder_layer(x, p, attn_fn, conv_buf, mk, mv):
    B, T, _ = x.shape
    x = x + 0.5 * swiglu(rmsnorm(x, p['ffn1_norm']), p['ffn1_w_gate'], p['ffn1_w_up'], p['ffn1_w_down'])
    h = rmsnorm(x, p['mix_norm'])
    z = h @ p['w_in']
    q, k, v, u_a, u_b, g_a, g_b = jnp.split(z, SPLIT_POINTS, axis=-1)
    to_heads = lambda a: a.reshape(B, T, N_GROUPS, HEADS_PER_GROUP, HEAD_DIM)
    attn, win_k, win_v = attn_fn(to_heads(q), to_heads(k), to_heads(v))
    a_branch = attn @ p['w_attn_proj']
    c_branch, conv_state = conformer_conv(u_a * jax.nn.sigmoid(u_b), conv_buf, p)
    merged = jax.nn.sigmoid(g_a) * a_branch + jax.nn.sigmoid(g_b) * c_branch
    x = x + merged @ p['w_o']
    x = x + cross_attention(rmsnorm(x, p['xattn_norm']), mk, mv, p['w_xq'], p['w_xo'])
    x = x + 0.5 * swiglu(rmsnorm(x, p['ffn2_norm']), p['ffn2_w_gate'], p['ffn2_w_up'], p['ffn2_w_down'])
    return x, win_k, win_v, conv_state


def setup_inputs(seed: int = 0) -> dict:
    key = jax.random.key(seed)
    keys = iter(jax.random.split(key, 64))
    f32 = jnp.float32

    def nrm(shape, scale):
        return jax.random.normal(next(keys), shape, f32) * scale

    def gain(shape):
        return 1.0 + nrm(shape, 0.01)

    Ld = DEPTH
    d = {}
    d['x_prompt'] = nrm((BATCH, SEQ, D_MODEL), 1.0)
    d['x_sample'] = nrm((DEC_BATCH, DEC_SEQ, D_MODEL), 1.0)
    d['mem_prompt'] = nrm((BATCH, N_MEM, D_MODEL), 1.0)
    for g, (window, _) in enumerate(GROUPS):
        lb = min(window, PAST_LEN)
        d['cache_win%d_k' % g] = nrm((Ld, DEC_BATCH, lb, HEADS_PER_GROUP, HEAD_DIM), 1.0)
        d['cache_win%d_v' % g] = nrm((Ld, DEC_BATCH, lb, HEADS_PER_GROUP, HEAD_DIM), 1.0)
    d['state_conv'] = nrm((Ld, DEC_BATCH, CONV_WIDTH - 1, CONV_DIM), 0.5)
    d['cache_mem_k'] = nrm((Ld, DEC_BATCH, N_MEM, X_HEADS, X_HEAD_DIM), 1.0)
    d['cache_mem_v'] = nrm((Ld, DEC_BATCH, N_MEM, X_HEADS, X_HEAD_DIM), 1.0)
    d['rel_bias'] = nrm((N_BUCKETS, N_ATTN_HEADS), 0.5)
    d['ffn1_norm'] = gain((Ld, D_MODEL))
    d['ffn1_w_gate'] = nrm((Ld, D_MODEL, D_FF), D_MODEL ** -0.5)
    d['ffn1_w_up'] = nrm((Ld, D_MODEL, D_FF), D_MODEL ** -0.5)
    d['ffn1_w_down'] = nrm((Ld, D_FF, D_MODEL), D_FF ** -0.5)
    d['mix_norm'] = gain((Ld, D_MODEL))
    d['w_in'] = nrm((Ld, D_MODEL, IN_WIDTH), D_MODEL ** -0.5)
    d['w_attn_proj'] = nrm((Ld, ATTN_OUT_WIDTH, D_MODEL), ATTN_OUT_WIDTH ** -0.5)
    d['conv_dw_w'] = nrm((Ld, CONV_WIDTH, CONV_DIM), CONV_WIDTH ** -0.5)
    d['conv_dw_b'] = nrm((Ld, CONV_DIM), 0.01)
    d['conv_ln_g'] = gain((Ld, CONV_DIM))
    d['conv_ln_b'] = nrm((Ld, CONV_DIM), 0.01)
    d['w_conv_proj'] = nrm((Ld, CONV_DIM, D_MODEL), CONV_DIM ** -0.5)
    d['w_o'] = nrm((Ld, D_MODEL, D_MODEL), D_MODEL ** -0.5)
    d['xattn_norm'] = gain((Ld, D_MODEL))
    d['mem_norm'] = gain((Ld, D_MODEL))
    d['w_xq'] = nrm((Ld, D_MODEL, D_MODEL), D_MODEL ** -0.5)
    d['w_xkv'] = nrm((Ld, D_MODEL, 2 * D_MODEL), D_MODEL ** -0.5)
    d['w_xo'] = nrm((Ld, D_MODEL, D_MODEL), D_MODEL ** -0.5)
    d['ffn2_norm'] = gain((Ld, D_MODEL))
    d['ffn2_w_gate'] = nrm((Ld, D_MODEL, D_FF), D_MODEL ** -0.5)
    d['ffn2_w_up'] = nrm((Ld, D_MODEL, D_FF), D_MODEL ** -0.5)
    d['ffn2_w_down'] = nrm((Ld, D_FF, D_MODEL), D_FF ** -0.5)
    d['final_norm'] = gain((D_MODEL,))
    return d


def reference(x_prompt, x_sample, mem_prompt,
              cache_win0_k, cache_win0_v, cache_win1_k, cache_win1_v, cache_win2_k, cache_win2_v,
              state_conv, cache_mem_k, cache_mem_v, rel_bias,
              ffn1_norm, ffn1_w_gate, ffn1_w_up, ffn1_w_down, mix_norm, w_in, w_attn_proj,
              conv_dw_w, conv_dw_b, conv_ln_g, conv_ln_b, w_conv_proj, w_o,
              xattn_norm, mem_norm, w_xq, w_xkv, w_xo,
              ffn2_norm, ffn2_w_gate, ffn2_w_up, ffn2_w_down, final_norm):
    stacked = dict(ffn1_norm=ffn1_norm, ffn1_w_gate=ffn1_w_gate, ffn1_w_up=ffn1_w_up,
                   ffn1_w_down=ffn1_w_down, mix_norm=mix_norm, w_in=w_in, w_attn_proj=w_attn_proj,
                   conv_dw_w=conv_dw_w, conv_dw_b=conv_dw_b, conv_ln_g=conv_ln_g, conv_ln_b=conv_ln_b,
                   w_conv_proj=w_conv_proj, w_o=w_o, xattn_norm=xattn_norm, mem_norm=mem_norm,
                   w_xq=w_xq, w_xkv=w_xkv, w_xo=w_xo, ffn2_norm=ffn2_norm,
                   ffn2_w_gate=ffn2_w_gate, ffn2_w_up=ffn2_w_up, ffn2_w_down=ffn2_w_down)
    cache_k = (cache_win0_k, cache_win1_k, cache_win2_k)
    cache_v = (cache_win0_v, cache_win1_v, cache_win2_v)
    xp, xs = x_prompt, x_sample
    pk, pv, pc, pmk, pmv = [], [], [], [], []
    sk, sv, sc = [], [], []
    for l in range(DEPTH):
        p = {name: arr[l] for name, arr in stacked.items()}
        mk_p, mv_p = memory_kv(mem_prompt, p)
        zero_buf = jnp.zeros((xp.shape[0], CONV_WIDTH - 1, CONV_DIM), xp.dtype)
        xp, wk, wv, cst = decoder_layer(
            xp, p, lambda q, k, v: dilated_attention_prompt(q, k, v, rel_bias), zero_buf, mk_p, mv_p)
        pk.append(wk); pv.append(wv); pc.append(cst); pmk.append(mk_p); pmv.append(mv_p)
        bk = [c[l] for c in cache_k]
        bv = [c[l] for c in cache_v]
        xs, wk, wv, cst = decoder_layer(
            xs, p, lambda q, k, v, bk=bk, bv=bv: dilated_attention_sample(q, k, v, bk, bv, rel_bias),
            state_conv[l], cache_mem_k[l], cache_mem_v[l])
        sk.append(wk); sv.append(wv); sc.append(cst)
    y_prompt = rmsnorm(xp, final_norm)
    y_sample = rmsnorm(xs, final_norm)
    stk = lambda lst, g: jnp.stack([e[g] for e in lst], axis=0)
    return (y_prompt, y_sample,
            stk(pk, 0), stk(pv, 0), stk(pk, 1), stk(pv, 1), stk(pk, 2), stk(pv, 2),
            jnp.stack(pc, axis=0), jnp.stack(pmk, axis=0), jnp.stack(pmv, axis=0),
            stk(sk, 0), stk(sv, 0), stk(sk, 1), stk(sv, 1), stk(sk, 2), stk(sv, 2),
            jnp.stack(sc, axis=0))
```

```python
import numpy as np
import concourse.bass as bass
import concourse.mybir as mybir
from concourse.bass_utils import run_bass_kernel_spmd

F32 = mybir.dt.float32
BF16 = mybir.dt.bfloat16
AF = mybir.ActivationFunctionType
ALU = mybir.AluOpType

NCORES = 8
D = 1024
S = 2048
NS_TOK = 32
NT = S + NS_TOK
DFF = 2816
NF = DFF // 128
EPS = 1e-6
GROUPS = ((128, 1), (512, 4), (2048, 16))
IN_WIDTH = 5376
NEG = -30000.0

ENGS = ("pe", "act", "dve", "pool", "sp")
NDSEM = {"sp": 44, "pool": 12, "act": 4}


class _Op:
    __slots__ = ("eng", "fn", "deps", "is_dma", "signal", "ticket", "sem", "prev_dma", "tag")

    def __init__(self, eng, fn, is_dma, tag):
        self.eng = eng
        self.fn = fn
        self.is_dma = is_dma
        self.deps = []
        self.signal = False
        self.ticket = 0
        self.sem = None
        self.prev_dma = None
        self.tag = tag


class Prog:
    def __init__(self, nc):
        self.nc = nc
        self.ops = {e: [] for e in ENGS}
        self.state = {}
        self.bar = []

    def barrier(self):
        b = []
        for e in ENGS:
            comp = [o for o in self.ops[e] if not o.is_dma]
            if comp:
                b.append(comp[-1])
            if e in NDSEM:
                dm = [o for o in self.ops[e] if o.is_dma]
                b.extend(dm[-NDSEM[e]:])
        self.bar = b
        self.state = {}

    def op(self, eng, fn, reads=(), writes=(), dma=False, tag=None):
        o = _Op(eng, fn, dma, tag)
        deps = {}

        def add(d, kind):
            if d is None or d is o:
                return
            if d.is_dma or o.is_dma:
                deps[id(d)] = d
            elif d.eng == eng:
                if eng != "pe" and kind != "war":
                    deps[id(d)] = d
            else:
                deps[id(d)] = d

        for k in reads:
            st = self.state.get(k)
            if st is not None:
                add(st[0], "raw")
            else:
                for d in self.bar:
                    add(d, "raw")
        for k in writes:
            st = self.state.get(k)
            if st is None:
                for d in self.bar:
                    add(d, "raw")
            if st is not None:
                add(st[0], "waw")
                for r in st[1].values():
                    add(r, "war")
                for r in st[2]:
                    add(r, "war")
        o.deps = list(deps.values())
        for k in reads:
            st = self.state.setdefault(k, [None, {}, []])
            if dma:
                st[2].append(o)
            else:
                st[1][eng] = o
        for k in writes:
            self.state[k] = [o, {}, []]
        self.ops[eng].append(o)
        return o

    def emit(self):
        nc = self.nc
        for e in ENGS:
            for o in self.ops[e]:
                for d in o.deps:
                    d.signal = True
        import contextlib
        with contextlib.ExitStack() as es:
            esem = {e: es.enter_context(nc.semaphore("s_" + e)) for e in ("pe", "act", "dve", "pool")}
            dsem = {e: [es.enter_context(nc.semaphore("d_%s%d" % (e, i))) for i in range(n)]
                    for e, n in NDSEM.items()}
            for e in ENGS:
                cnt = 0
                k = 0
                hist = []
                for o in self.ops[e]:
                    if o.is_dma:
                        n = NDSEM[e]
                        o.sem = dsem[e][k % n]
                        o.ticket = 16 * (k // n + 1)
                        o.prev_dma = hist[k - n] if k >= n else None
                        hist.append(o)
                        k += 1
                    elif o.signal:
                        cnt += 1
                        o.ticket = cnt
                        o.sem = esem[e]
            block = es.enter_context(nc.Block())

            def run(e, eng):
                waited = {}
                lastd = {}
                for o in self.ops[e]:
                    waits = {}
                    for d in o.deps:
                        key = id(d.sem)
                        if key not in waits or waits[key][1] < d.ticket:
                            waits[key] = (d.sem, d.ticket)
                    if o.prev_dma is not None:
                        d = o.prev_dma
                        key = id(d.sem)
                        if key not in waits or waits[key][1] < d.ticket:
                            waits[key] = (d.sem, d.ticket)
                    for key, (sem, val) in waits.items():
                        if waited.get(key, 0) < val:
                            eng.wait_ge(sem, val)
                            waited[key] = val
                    inst = o.fn(eng)
                    if o.is_dma:
                        inst.then_inc(o.sem, 16)
                        lastd[id(o.sem)] = (o.sem, o.ticket)
                    elif o.signal:
                        inst.then_inc(o.sem, 1)
                for key, (sem, val) in lastd.items():
                    if waited.get(key, 0) < val:
                        eng.wait_ge(sem, val)

            @block.tensor
            def _(eng):
                run("pe", eng)

            @block.scalar
            def _(eng):
                run("act", eng)

            @block.vector
            def _(eng):
                run("dve", eng)

            @block.gpsimd
            def _(eng):
                run("pool", eng)

            @block.sync
            def _(eng):
                run("sp", eng)


class K:
    def __init__(self, stage="full"):
        self.stage = stage
        self.nc = bass.Bass("TRN2", target_bir_lowering=False)
        self.P = Prog(self.nc)
        self.dram = {}
        self._uid = 0

    def din(self, name, shape, dt=F32):
        t = self.nc.dram_tensor(name, list(shape), dt, kind="ExternalInput").ap()
        self.dram[name] = t
        return t

    def dout(self, name, shape, dt=F32):
        t = self.nc.dram_tensor(name, list(shape), dt, kind="ExternalOutput").ap()
        self.dram[name] = t
        return t

    def dscratch(self, name, shape, dt=F32):
        t = self.nc.dram_tensor(name, list(shape), dt, kind="Internal").ap()
        self.dram[name] = t
        return t

    def sb(self, name, shape, dt):
        return self.es.enter_context(self.nc.sbuf_tensor(name, list(shape), dt))

    def dma(self, q, out, in_, reads=(), writes=(), **kw):
        return self.P.op(q, lambda e: e.dma_start(out=out, in_=in_, **kw), reads, writes, dma=True)

    def mm(self, out, lhsT, rhs, start, stop, reads=(), writes=(), **kw):
        return self.P.op("pe", lambda e: e.matmul(out, lhsT, rhs, start=start, stop=stop, **kw),
                         reads, writes)

    def tr(self, out, in_, ident, reads=(), writes=()):
        return self.P.op("pe", lambda e: e.transpose(out, in_, ident), reads, writes)

    def act(self, out, in_, func, reads=(), writes=(), **kw):
        return self.P.op("act", lambda e: e.activation(out=out, in_=in_, func=func, **kw), reads, writes)

    def tt(self, eng, out, in0, in1, op, reads=(), writes=()):
        return self.P.op(eng, lambda e: e.tensor_tensor(out=out, in0=in0, in1=in1, op=op), reads, writes)

    def ts(self, eng, out, in0, s1, s2, op0, op1=None, reads=(), writes=()):
        if op1 is None:
            return self.P.op(eng, lambda e: e.tensor_scalar(out=out, in0=in0, scalar1=s1, scalar2=None, op0=op0),
                             reads, writes)
        return self.P.op(eng, lambda e: e.tensor_scalar(out=out, in0=in0, scalar1=s1, scalar2=s2, op0=op0, op1=op1),
                         reads, writes)

    def stt(self, out, in0, scalar, in1, op0, op1, reads=(), writes=()):
        return self.P.op("dve", lambda e: e.scalar_tensor_tensor(out=out, in0=in0, scalar=scalar, in1=in1,
                                                                 op0=op0, op1=op1), reads, writes)

    def copy(self, eng, out, in_, reads=(), writes=()):
        if eng == "act":
            return self.P.op("act", lambda e: e.copy(out=out, in_=in_), reads, writes)
        return self.P.op(eng, lambda e: e.tensor_copy(out=out, in_=in_), reads, writes)

    def memset(self, eng, ap, val, writes=()):
        return self.P.op(eng, lambda e: e.memset(ap, val), (), writes)


CHUNKS = [(0, 512), (512, 512), (1024, 512), (1536, 512), (2048, 32)]
WNAMES = ["ffn1_norm", "ffn1_w_gate", "ffn1_w_up", "ffn1_w_down", "mix_norm", "w_in", "w_attn_proj",
          "conv_dw_w", "conv_dw_b", "conv_ln_g", "conv_ln_b", "w_conv_proj", "w_o", "xattn_norm", "mem_norm",
          "w_xq", "w_xkv", "w_xo", "ffn2_norm", "ffn2_w_gate", "ffn2_w_up", "ffn2_w_down", "final_norm", "rel_bias"]
WSHAPES = {"ffn1_norm": [1, D], "ffn1_w_gate": [1, D, DFF], "ffn1_w_up": [1, D, DFF], "ffn1_w_down": [1, DFF, D],
           "mix_norm": [1, D], "w_in": [1, D, IN_WIDTH], "w_attn_proj": [1, 256, D], "conv_dw_w": [1, 31, 512],
           "conv_dw_b": [1, 512], "conv_ln_g": [1, 512], "conv_ln_b": [1, 512], "w_conv_proj": [1, 512, D],
           "w_o": [1, D, D], "xattn_norm": [1, D], "mem_norm": [1, D], "w_xq": [1, D, D], "w_xkv": [1, D, 2 * D],
           "w_xo": [1, D, D], "ffn2_norm": [1, D], "ffn2_w_gate": [1, D, DFF], "ffn2_w_up": [1, D, DFF],
           "ffn2_w_down": [1, DFF, D], "final_norm": [D], "rel_bias": [32, 12]}
NORM_IDX = {"ffn1_norm": 0, "mix_norm": 1, "xattn_norm": 2, "ffn2_norm": 3, "final_norm": 4, "mem_norm": 5}


class Arena:
    def __init__(self, k, nbytes):
        self.t = k.sb("arena", [128, nbytes // 2], BF16)
        self.n = nbytes
        self.off = 0

    def reset(self, off=0):
        self.off = off

    def alloc(self, shape, dt):
        esz = 4 if dt == F32 else 2
        n = esz
        for d in shape[1:]:
            n *= d
        n = (n + 63) // 64 * 64
        assert self.off + n <= self.n, ("arena overflow", self.off, n, self.n)
        v = self.t[:, self.off // 2:(self.off + n) // 2]
        self.off += n
        if dt == F32:
            v = v.bitcast(F32)
        tot = 1
        for d in shape[1:]:
            tot *= d
        v = v[:, 0:tot]
        if len(shape) == 3:
            v = v.rearrange("p (a b) -> p a b", a=shape[1])
        elif len(shape) == 4:
            v = v.rearrange("p (a b c) -> p a b c", a=shape[1], b=shape[2])
        elif len(shape) == 5:
            v = v.rearrange("p (a b c d) -> p a b c d", a=shape[1], b=shape[2], c=shape[3])
        return v[0:shape[0]]


class Builder(K):
    def build(self):
        import contextlib
        k = self
        nc = self.nc
        P = self.P
        xp = k.din("xp", [S, D])
        xs = k.din("xs", [NS_TOK, D])
        mem = k.din("mem", [256, D])
        cwk = [k.din("cw%dk" % g, [4, GROUPS[g][0], 256]) for g in range(3)]
        cwv = [k.din("cw%dv" % g, [4, GROUPS[g][0], 256]) for g in range(3)]
        stc = k.din("stc", [4, 30, 512])
        cmk = k.din("cmk", [4, 256, D])
        cmv = k.din("cmv", [4, 256, D])
        W = {n: k.din(n, WSHAPES[n]) for n in WNAMES}
        k.W = W
        ebuck = k.din("ebuck", [3, 33, 384])
        biasrep = k.dscratch("biasrep", [12, 128, 384])
        k.biasrep = biasrep
        yp = k.dout("yp", [S, D])
        ys = k.dout("ys", [NS_TOK, D])
        pwk = [k.dout("pw%dk" % g, [GROUPS[g][0], 256]) for g in range(3)]
        pwv = [k.dout("pw%dv" % g, [GROUPS[g][0], 256]) for g in range(3)]
        pconv = k.dout("pconv", [30, 512])
        pmk = k.dout("pmk", [256, D])
        pmv = k.dout("pmv", [256, D])
        swk = [k.dout("sw%dk" % g, [4, GROUPS[g][0], 256]) for g in range(3)]
        swv = [k.dout("sw%dv" % g, [4, GROUPS[g][0], 256]) for g in range(3)]
        sconv = k.dout("sconv", [4, 30, 512])
        k.io = dict(xp=xp, xs=xs, mem=mem, cwk=cwk, cwv=cwv, stc=stc, cmk=cmk, cmv=cmv, yp=yp, ys=ys,
                    pwk=pwk, pwv=pwv, pconv=pconv, pmk=pmk, pmv=pmv, swk=swk, swv=swv, sconv=sconv)
        xspill = k.dscratch("xspill", [128, 8 * NT])

        with contextlib.ExitStack() as es:
            k.es = es
            k.xT = k.sb("xT", [128, 8, NT], F32)
            k.identf = k.sb("identf", [128, 128], F32)
            k.ident = k.sb("ident", [128, 128], BF16)
            k.ones = k.sb("ones", [128, 128], BF16)
            k.mhalf = k.sb("mhalf", [128, 8], F32)
            k.onesf = k.sb("onesf", [128, 128], F32)
            k.rsc = [k.sb("rsc%d" % i, [128, 4], F32) for i in range(2)]
            k.rdg = [k.sb("rdg%d" % i, [128, 4, 128], F32) for i in range(2)]
            k.gcol = k.sb("gcol", [128, 6, 8], F32)
            k.ar = Arena(k, 137 * 1024)
            k.ps = es.enter_context(nc.psum_tensor("psall", [128, 8, 512], F32))
            k._rot = {}

            k.memset("pool", k.identf[:], 0.0, writes=["identf"])
            P.op("pool", lambda e: e.affine_select(out=k.identf[:], in_=k.identf[:], pattern=[[-1, 128]],
                                                   compare_op=ALU.not_equal, fill=1.0, base=0,
                                                   channel_multiplier=1), reads=["identf"], writes=["identf"])
            k.copy("dve", k.ident[:], k.identf[:], reads=["identf"], writes=["ident"])
            k.memset("pool", k.ones[:], 1.0, writes=["ones"])
            k.memset("pool", k.mhalf[:], -0.5, writes=["mhalf"])
            k.memset("pool", k.onesf[:], 1.0, writes=["onesf"])
            for name, gi in NORM_IDX.items():
                src = W[name]
                src = src[0, :] if len(WSHAPES[name]) == 2 else src
                k.dma("sp", k.gcol[:, gi, :], src.rearrange("(kc p) -> p kc", p=128), writes=[("gcol", gi)],
                      allow_slow_non_contiguous=True)

            BOOT0 = 33280 + 8192 + 46464 + 24576
            k.ar.reset(BOOT0)
            stg = [k.ar.alloc([128, D], F32) for _ in range(2)]
            k.boot_mark = k.ar.off
            for t in range(17):
                R = 128 if t < 16 else 32
                sgi = t % 2
                src = xp[t * 128:(t + 1) * 128, :] if t < 16 else xs[:, :]
                k.dma("sp", stg[sgi][0:R, :], src, reads=["boot"], writes=[("stg", sgi)])
                bp = 6 if t % 2 == 0 else 4
                for kc in range(8):
                    k.tr(k.ps[:, bp + kc // 4, (kc % 4) * 128:(kc % 4) * 128 + R],
                         stg[sgi][0:R, kc * 128:(kc + 1) * 128], k.identf[0:R, 0:R],
                         reads=[("stg", sgi), "identf", "boot"], writes=[("ps", bp + kc // 4)])
                src_ps = k.ps[:, bp:bp + 2, :].rearrange("p b (k r) -> p (b k) r", k=4)[:, :, 0:R]
                eng = "act" if t % 2 == 0 else "dve"
                k.copy(eng, k.xT[:, :, t * 128:t * 128 + R], src_ps, reads=[("ps", bp), ("ps", bp + 1)],
                       writes=[("xT", kc, min(t // 4, 4)) for kc in range(8)])

            if k.stage in ("mix", "full"):
                k.setup_mix(ebuck)
            if k.stage not in ("ffn", "full"):
                P.barrier()
            if k.stage in ("ffn", "full"):
                k.ffn("ffn1")
                P.barrier()
            if k.stage in ("mix", "full"):
                k.mix()
                P.barrier()
            if k.stage in ("xattn", "full"):
                k.xattn()
                P.barrier()
            if k.stage in ("ffn", "full"):
                k.ffn("ffn2")
                P.barrier()
            k.final_norm()
            P.emit()
        return nc

    def rot(self, name, n):
        i = self._rot.get(name, 0)
        self._rot[name] = i + 1
        return i % n


    def rstd_bc(self, src, srckey, ci, cs, n, sq, banks=(6, 7)):
        k = self
        i = k.rot("sq", len(sq))
        k.act(sq[i][:, :, 0:n], src[:, :, cs:cs + n], AF.Square,
              reads=[(srckey, kc, ci) for kc in range(8)], writes=[("sq", i)])
        nt = (n + 127) // 128
        R = min(n, 128)
        b = banks[k.rot("psn%s" % (banks,), len(banks))]
        for c in range(nt):
            for kc in range(8):
                k.mm(k.ps[0:R, b, c:c + 1], sq[i][:, kc, c * 128:c * 128 + R], k.ones[:, 0:1], kc == 0, kc == 7,
                     reads=[("sq", i), "ones"], writes=[("ps", b)])
        j = k.rot("rsc", 2)
        rsc = k.rsc[j]
        k.ts("dve", rsc[0:R, 0:nt], k.ps[0:R, b, 0:nt], 1.0 / D, EPS, ALU.mult, ALU.add,
             reads=[("ps", b)], writes=[("rsc", j)])
        k.tt("pool", rsc[0:R, 0:nt], rsc[0:R, 0:nt], k.mhalf[0:R, 0:nt], ALU.pow,
             reads=[("rsc", j), "mhalf"], writes=[("rsc", j)])
        for c in range(nt):
            k.ts("dve", k.rdg[j][0:R, c, 0:R], k.identf[0:R, 0:R], rsc[0:R, c:c + 1], None, ALU.mult,
                 reads=["identf", ("rsc", j)], writes=[("rdg", j, c)])
        for c in range(nt):
            k.mm(k.ps[:, b, c * 128:c * 128 + R], k.onesf[0:R, :], k.rdg[j][0:R, c, 0:R], True, True,
                 reads=["onesf", ("rdg", j, c)], writes=[("ps", b)])
        return k.ps[:, b, 0:n], ("ps", b)

    def norm(self, gi, hT, sq, rstd, chunks=CHUNKS, src=None, keyp="hT", srckey="xT", final=False):
        k = self
        src = k.xT if src is None else src
        for ci, (cs, n) in enumerate(chunks):
            rb, rkey = k.rstd_bc(src, srckey, ci, cs, n, sq)
            for kc in range(8):
                k.stt(hT[:, kc, cs:cs + n], src[:, kc, cs:cs + n], k.gcol[:, gi, kc:kc + 1], rb,
                      ALU.mult, ALU.mult, reads=[(srckey, kc, ci), rkey, ("gcol", gi)],
                      writes=[(keyp, kc, ci)])

    def ffn(self, pre):
        k = self
        W = k.W
        wg, wu, wd = W[pre + "_w_gate"], W[pre + "_w_up"], W[pre + "_w_down"]
        ar = k.ar
        ar.reset()
        hT = ar.alloc([128, 8, NT], BF16)
        sq = [ar.alloc([128, 8, 512], BF16) for _ in range(1)]
        rstd = None
        actb = ar.alloc([128, NF, 1056], BF16)
        wgs = [ar.alloc([128, 8, 256], BF16) for _ in range(3)]
        wus = [ar.alloc([128, 8, 256], BF16) for _ in range(3)]
        wds = [ar.alloc([128, NF, 256], BF16) for _ in range(2)]
        sg = [ar.alloc([128, 512], F32) for _ in range(2)]

        def load_gu(s):
            wi = k.rot("wgu_l", 3)
            k.dma("pool", wgs[wi][:], wg[0, :, s * 256:(s + 1) * 256].rearrange("(kc p) f -> p kc f", p=128),
                  writes=[("wg", wi)])
            k.dma("pool", wus[wi][:], wu[0, :, s * 256:(s + 1) * 256].rearrange("(kc p) f -> p kc f", p=128),
                  writes=[("wu", wi)])
        nsl = NF // 2
        seq = [s_ for _ in range(2) for s_ in range(nsl)]
        nload = 0
        for _ in range(2):
            load_gu(seq[nload])
            nload += 1
        k.norm(NORM_IDX[pre + "_norm"], hT, sq, rstd)
        blocks = [(0, [0, 1]), (1024, [2, 3, 4])]
        for bi, (c0, cis) in enumerate(blocks):
            for s in range(NF // 2):
                wi = k.rot("wgu", 3)
                if nload < len(seq):
                    load_gu(seq[nload])
                    nload += 1
                for j in range(2):
                    f = 2 * s + j
                    for ci in cis:
                        cs, n = CHUNKS[ci]
                        bg = k.rot("bg", 2)
                        bu = 2 + k.rot("bu", 2)
                        for kc in range(8):
                            k.mm(k.ps[:, bg, 0:n], wgs[wi][:, kc, j * 128:(j + 1) * 128], hT[:, kc, cs:cs + n],
                                 kc == 0, kc == 7, reads=[("wg", wi), ("hT", kc, ci)], writes=[("ps", bg)])
                        for kc in range(8):
                            k.mm(k.ps[:, bu, 0:n], wus[wi][:, kc, j * 128:(j + 1) * 128], hT[:, kc, cs:cs + n],
                                 kc == 0, kc == 7, reads=[("wu", wi), ("hT", kc, ci)], writes=[("ps", bu)])
                        si = k.rot("sg", 2)
                        k.act(sg[si][:, 0:n], k.ps[:, bg, 0:n], AF.Silu, reads=[("ps", bg)], writes=[("sg", si)])
                        k.tt("dve", actb[:, f, cs - c0:cs - c0 + n], sg[si][:, 0:n], k.ps[:, bu, 0:n], ALU.mult,
                             reads=[("sg", si), ("ps", bu)], writes=[("act", f, ci)])
            for s in range(4):
                wi = k.rot("wd", 2)
                fence = ["boot"] if (pre == "ffn1" and bi == 0 and s < 2) else []
                k.dma("pool", wds[wi][:], wd[0, :, s * 256:(s + 1) * 256].rearrange("(f p) c -> p f c", p=128),
                      writes=[("wd", wi)] + fence)
                for j in range(2):
                    oc = 2 * s + j
                    for ci in cis:
                        cs, n = CHUNKS[ci]
                        bd = 4 + k.rot("bd", 2)
                        for f in range(NF):
                            k.mm(k.ps[:, bd, 0:n], wds[wi][:, f, j * 128:(j + 1) * 128],
                                 actb[:, f, cs - c0:cs - c0 + n], f == 0, f == NF - 1,
                                 reads=[("wd", wi), ("act", f, ci)], writes=[("ps", bd)])
                        k.stt(k.xT[:, oc, cs:cs + n], k.ps[:, bd, 0:n], 0.5, k.xT[:, oc, cs:cs + n],
                              ALU.mult, ALU.add, reads=[("ps", bd), ("xT", oc, ci)], writes=[("xT", oc, ci)])


    def setup_mix(self, ebuck):
        k = self
        P = k.P
        ar = k.ar
        io = k.io
        ar.reset(k.boot_mark)
        rb = ar.alloc([33, 12], F32)
        rbb = ar.alloc([33, 12, 128], F32)
        eb = ar.alloc([33, 3, 384], F32)
        wrep = [ar.alloc([128, 384], F32) for _ in range(2)]
        P.op("pool", lambda e: e.memset(rb[:], 1.0), ["boot"], ["rb"])
        k.dma("sp", rb[0:32, :], k.W["rel_bias"][:, :], writes=["rb"], reads=["rb", "boot"])
        k.dma("sp", eb[:], ebuck.rearrange("g b u -> b g u"), reads=["boot"], writes=["eb"])
        k.copy("dve", rbb[:], rb[:].unsqueeze(2).to_broadcast([33, 12, 128]), reads=["rb", "boot"], writes=["rbb"])
        for c in range(12):
            g = c // 4
            b = c % 2
            k.mm(k.ps[:, b, 0:384], rbb[:, c, :], eb[:, g, :], True, True, reads=["rbb", "eb", "boot"], writes=[("ps", b)])
            k.copy("act", wrep[b][:], k.ps[:, b, 0:384], reads=[("ps", b), "boot"], writes=[("wrep", b)])
            k.dma("sp", k.biasrep[c], wrep[b][:], reads=[("wrep", b), "boot"], writes=[("biasrep", c)])

    def shift_copies(self):
        k = self
        io = k.io
        jobs = []
        for g in (2, 1, 0):
            Lb = GROUPS[g][0]
            for bb in range(4):
                for src, dst in ((io["cwk"][g], io["swk"][g]), (io["cwv"][g], io["swv"][g])):
                    jobs.append((dst[bb, 0:Lb - 8, :].rearrange("(a b) c -> a (b c)", b=8),
                                 src[bb, 8:Lb, :].rearrange("(a b) c -> a (b c)", b=8)))
        for bb in range(4):
            jobs.append((io["sconv"][bb, 0:22, :], io["stc"][bb, 8:30, :]))
        return jobs

    def mix(self):
        k = self
        P = k.P
        W = k.W
        io = k.io
        ar = k.ar
        win = W["w_in"][0]

        def sl(s0, n, st):
            return slice(s0, s0 + (n - 1) * st + 1, st)

        import os
        msub = 9
        if msub < 1:
            return
        ar.reset()
        hT = ar.alloc([128, 8, NT], BF16)
        oT = ar.alloc([128, 2, NT], BF16)
        mark0 = ar.off
        sq = [ar.alloc([128, 8, 512], BF16)]
        rstd = [ar.alloc([128, 512], F32) for _ in range(2)]
        k.norm(NORM_IDX["mix_norm"], hT, sq, rstd)
        P.barrier()
        ar.reset(mark0)
        NZ = ar.alloc([128, 2, 2, NT], F32)
        mark1 = ar.off
        k.wslab = [ar.alloc([128, 8, 256], BF16) for _ in range(2)]
        QT = ar.alloc([128, 2, NT], BF16)
        KT = ar.alloc([128, 2, S], BF16)
        Vp = ar.alloc([128, 16, 4, 128], BF16)
        bias = ar.alloc([128, 2, 2, 2, 128], F32)
        PTb = [ar.alloc([128, 512], BF16) for _ in range(4)]
        kvw = oT.rearrange("p a t -> p (a t)")[:, 0:4096].rearrange("p (k c) -> p k c", k=8)
        kvst = [ar.alloc([128, 512], F32) for _ in range(2)]
        ktok = [ar.alloc([128, 256], BF16) for _ in range(3)]
        opad = ar.alloc([128, 2, 128], BF16)
        ks_bf = ar.alloc([32, 256], BF16)
        vs_bf = ar.alloc([32, 256], BF16)
        KsT = ar.alloc([128, 2, 32], BF16)
        KcT8 = ar.alloc([128, 8, 2, 128], BF16)
        Vpn = [ar.alloc([8, 4, 128], BF16) for _ in range(2)]
        sbias = ar.alloc([128, 2, 9, 2, 8], F32)
        stmp = ar.alloc([8, 2, 2, 2, 8], F32)
        vp_flat = Vp.rearrange("p a b c -> p (a b c)")
        kt_flat = KT.rearrange("p a t -> p (a t)")
        Vpc8 = [vp_flat[:, 0:4096].rearrange("p (c h x) -> p c h x", c=8, h=4),
                kt_flat[:, 0:4096].rearrange("p (c h x) -> p c h x", c=8, h=4)]
        kc8 = [vp_flat[:, 4096:6144].rearrange("p (c x) -> p c x", c=8),
               QT[:, 0, 0:2048].rearrange("p (c x) -> p c x", c=8)]
        vc8 = [vp_flat[:, 6144:8192].rearrange("p (c x) -> p c x", c=8),
               QT[:, 1, 0:2048].rearrange("p (c x) -> p c x", c=8)]

        k.memset("pool", Vp[:], 0.0, writes=[("Vp", b) for b in range(16)])
        for i in range(2):
            k.memset("pool", Vpn[i][:], 0.0, writes=[("Vpn", i)])
        k.memset("pool", opad[:], 0.0, writes=["opad"])
        k.memset("pool", opad[:, 0, 0:64], 1.0, writes=["opad"])
        k.memset("pool", opad[:, 1, 64:128], 1.0, writes=["opad"])

        def pad_copy(eng, dst, src256, rows, rkeys, wkeys):
            d2 = dst[0:rows].rearrange("p (a b) c -> p a b c", b=2)
            s2 = src256[0:rows].rearrange("p (a b d) -> p a b d", a=2, b=2)
            for par in range(2):
                k.copy(eng, d2[:, :, par, par * 64:(par + 1) * 64], s2[:, :, par, :], reads=rkeys, writes=wkeys)

        def attn_block(g_first, qap, nq, keyblocks, evac_cols, qkeys, tag, fullbias=None):
            st_ = k.rot("aset", 2)
            sbanks = (0, 1) if st_ == 0 else (4, 5)
            nb = len(keyblocks)
            for par in range(2):
                bk = sbanks[par]
                for bi, (nk, ktap, vp, bvf, rkeys) in enumerate(keyblocks):
                    for a in range(2):
                        h = 2 * a + par
                        sl0 = (bi * 2 + a) * nq
                        k.mm(k.ps[0:nk, bk, sl0:sl0 + nq], ktap(h), qap(h), True, True,
                             reads=list(rkeys) + list(qkeys), writes=[("ps", bk)])
                same = all(kb[0] == keyblocks[0][0] for kb in keyblocks)
                if fullbias is not None:
                    ncol = nb * 2 * nq
                    pv = k.ps[:, bk, 0:ncol]
                    k.tt("dve", pv, pv, fullbias(par), ALU.add, reads=[("ps", bk), "sbias"], writes=[("ps", bk)])
                    k.act(PTb[st_ * 2 + par][:, 0:ncol], pv, AF.Exp, reads=[("ps", bk)], writes=[("PTb", st_, par)])
                elif same and nb == 2 and nq == 128:
                    nk = keyblocks[0][0]
                    pv = k.ps[0:nk, bk, 0:4 * nq]
                    k.tt("dve", pv, pv, bias[:, par, :, :, :].rearrange("p w a q -> p (w a q)"), ALU.add,
                         reads=[("ps", bk), "bias"], writes=[("ps", bk)])
                    k.act(PTb[st_ * 2 + par][0:nk, 0:4 * nq], pv, AF.Exp, reads=[("ps", bk)],
                          writes=[("PTb", st_, par)])
                else:
                    for bi, (nk, ktap, vp, bvf, rkeys) in enumerate(keyblocks):
                        pv = k.ps[0:nk, bk, bi * 2 * nq:(bi * 2 + 2) * nq].rearrange("p (a q) -> p a q", a=2)
                        k.tt("dve", pv, pv, bvf(par), ALU.add, reads=[("ps", bk), "bias"], writes=[("ps", bk)])
                        k.act(PTb[st_ * 2 + par][0:nk, bi * 2 * nq:(bi * 2 + 2) * nq],
                              k.ps[0:nk, bk, bi * 2 * nq:(bi * 2 + 2) * nq], AF.Exp, reads=[("ps", bk)],
                              writes=[("PTb", st_, par)])
            return (st_, g_first, nq, keyblocks, evac_cols)

        def attn_B(ctx):
            st_, g_first, nq, keyblocks, evac_cols = ctx
            pvb = 6 + st_
            nb = len(keyblocks)
            for nz in range(2):
                for pair in range(2):
                    first = True
                    col = (nz * 2 + pair) * nq
                    for bi, (nk, ktap, vp, bvf, rkeys) in enumerate(keyblocks):
                        for hh in range(2):
                            h = pair * 2 + hh
                            lhs = vp[0:nk, h, :] if nz == 0 else opad[0:nk, hh, :]
                            last = (bi == nb - 1) and hh == 1
                            pcol = (bi * 2 + pair) * nq
                            k.mm(k.ps[:, pvb, col:col + nq], lhs, PTb[st_ * 2 + hh][0:nk, pcol:pcol + nq], first, last,
                                 reads=list(rkeys) + [("PTb", st_, hh), "opad"], writes=[("ps", pvb)])
                            first = False
            src = k.ps[:, pvb, 0:4 * nq].rearrange("p (a b q) -> p a b q", a=2, b=2)
            dst = evac_cols
            k._pace = k._pace + 1 if hasattr(k, "_pace") else 0
            if g_first:
                k.copy("dve", dst, src, reads=[("ps", pvb)], writes=["NZ", ("pace", k._pace)])
            else:
                k.tt("dve", dst, src, dst, ALU.add, reads=[("ps", pvb), "NZ"], writes=["NZ", ("pace", k._pace)])

        for g, (win_len, d) in enumerate(GROUPS):
            L = S // d
            for which in range(2):
                for par in range(2):
                    src = bass.AP(tensor=k.biasrep.tensor,
                                  offset=(4 * g + par) * 128 * 384 + (127 if which == 0 else 255),
                                  ap=[[383, 128], [2 * 128 * 384, 2], [1, 128]])
                    k.dma("sp", bias[:, par, which, :, :], src, reads=["biasrep"], writes=["bias"])
            if g == 0:
                shift_jobs = k.shift_copies()
            C = min(d, 8)
            nqc = 8 // C
            NB = C + 1
            k.memset("pool", sbias[:], NEG, writes=["sbias"])
            for r in range(C):
                k.copy("dve", sbias[:, :, r, :, r:8:d] if d < 8 else sbias[:, :, r, :, r:r + 1],
                       bias[:, :, 1, :, 0:nqc], reads=["bias", "sbias"], writes=["sbias"])
            cur8 = bias[0:8, :, 0, :, 0:8]
            if d == 1:
                k.copy("dve", sbias[0:8, :, C, :, :], cur8, reads=["bias", "sbias"], writes=["sbias"])
            else:
                k.copy("dve", stmp[:, 0], cur8, reads=["bias"], writes=["stmp0"])
                st0 = stmp[:, 0].rearrange("p a b q -> p (a b) q")
                P.op("pool", lambda e: e.affine_select(out=st0, in_=st0, pattern=[[0, 4], [-1, 8]],
                                                       compare_op=ALU.is_equal, fill=NEG, base=0, channel_multiplier=1),
                     reads=["stmp0"], writes=["stmp0"])
                if d == 4:
                    k.memset("pool", stmp[:, 1], NEG, writes=["stmp1"])
                    k.copy("dve", stmp[:, 1, :, :, 3:8], bias[0:8, :, 0, :, 0:5], reads=["bias", "stmp1"], writes=["stmp1"])
                    st1 = stmp[:, 1].rearrange("p a b q -> p (a b) q")
                    P.op("pool", lambda e: e.affine_select(out=st1, in_=st1, pattern=[[0, 4], [-1, 8]],
                                                           compare_op=ALU.is_equal, fill=NEG, base=4, channel_multiplier=1),
                         reads=["stmp1"], writes=["stmp1"])
                    k.tt("dve", sbias[0:8, :, C, :, :], stmp[:, 0], stmp[:, 1], ALU.max, reads=["stmp0", "stmp1", "sbias"],
                         writes=["sbias"])
                else:
                    k.copy("dve", sbias[0:8, :, C, :, :], stmp[:, 0], reads=["stmp0", "sbias"], writes=["sbias"])

            def evac_q(oc, ci, cs, n, b, d=d, L=L):
                if ci < 4:
                    dst = QT[:, oc, 0:S].rearrange("p (r j) -> p r j", r=d)[:, :, cs // d:(cs + n) // d]
                    srcp = k.ps[:, b, 0:n].rearrange("p (jj r) -> p r jj", r=d)
                else:
                    dst = QT[:, oc, cs:cs + n]
                    srcp = k.ps[:, b, 0:n]
                k.act(dst, srcp, AF.Copy, reads=[("ps", b)], writes=[("QT", oc, ci)], scale=0.125)
            wi_q = k.rot("wq", 2)
            wtq = k.wslab[wi_q]
            k.dma("pool", wtq[:, :, 0:256], win[:, g * 256:(g + 1) * 256].rearrange("(kc p) f -> p kc f", p=128),
                  writes=[("wslab", wi_q)])
            k.dma("pool", kvw[:, :, 0:256], win[:, 768 + g * 256:768 + (g + 1) * 256].rearrange("(kc p) f -> p kc f", p=128),
                  writes=["kvw"])
            k.dma("pool", kvw[:, :, 256:512], win[:, 1536 + g * 256:1536 + (g + 1) * 256].rearrange("(kc p) f -> p kc f", p=128),
                  writes=["kvw"], reads=["kvw"])
            keep0 = S - win_len
            qunits = [(oc, ci) for oc in range(2) for ci in range(5)]

            def q_unit(oc, ci):
                cs, n = CHUNKS[ci]
                b = k.rot("qb", 2)
                for kc in range(8):
                    k.mm(k.ps[:, b, 0:n], wtq[:, kc, oc * 128:(oc + 1) * 128], hT[:, kc, cs:cs + n], kc == 0, kc == 7,
                         reads=[("wslab", wi_q), ("hT", kc, ci)], writes=[("ps", b)])
                evac_q(oc, ci, cs, n, b)

            def kt_transposes(blk, ti, r, b_):
                psb = k.ps[:, 7, :].bitcast(BF16)
                for pair in range(2):
                    k.tr(psb[:, pair * 128:(pair + 1) * 128], ktok[ti][:, pair * 128:(pair + 1) * 128], k.ident[:],
                         reads=[("ktok", ti), "ident"], writes=[("ps", 7)])
                kpos = r * L + 128 * b_
                k.copy("dve", KT[:, :, kpos:kpos + 128], psb[:, 0:256].rearrange("p (a m) -> p a m", a=2),
                       reads=[("ps", 7)], writes=[("KT", blk)])
            lag = None
            for blk in range(17):
                if blk < 16:
                    r, b_ = blk // (16 // d), blk % (16 // d)
                    rows = 128
                    tsel = sl(r + 128 * b_ * d, 128, d)
                else:
                    rows = 32
                    tsel = slice(S, S + 32)
                pb = 2 + k.rot("pbkv2", 2)
                for kc in range(8):
                    k.mm(k.ps[0:rows, pb, :], hT[:, kc, tsel], kvw[:, kc, :], kc == 0, kc == 7,
                         reads=["kvw"] + [("hT", kc, ci) for ci in range(5)], writes=[("ps", pb)])
                si = k.rot("kvst", 2)
                k.copy("act", kvst[si][0:rows, :], k.ps[0:rows, pb, :], reads=[("ps", pb)], writes=[("kvst", si)])
                if qunits:
                    q_unit(*qunits.pop(0))
                if lag is not None:
                    kt_transposes(*lag)
                    lag = None
                if blk < 16:
                    t0 = r + 128 * b_ * d
                    if t0 >= keep0:
                        k.dma("sp", io["pwk"][g][sl(t0 - keep0, 128, d), :], kvst[si][:, 0:256], reads=[("kvst", si)])
                        k.dma("sp", io["pwv"][g][sl(t0 - keep0, 128, d), :], kvst[si][:, 256:512], reads=[("kvst", si)])
                    ti = k.rot("ktok", 3)
                    k.copy("dve", ktok[ti][:], kvst[si][:, 0:256], reads=[("kvst", si)], writes=[("ktok", ti)])
                    pad_copy("pool", Vp[:, blk], kvst[si][:, 256:512], 128, [("kvst", si)], [("Vp", blk)])
                    lag = (blk, ti, r, b_)
                else:
                    for bb in range(4):
                        Lb = win_len
                        k.dma("sp", io["swk"][g][bb, Lb - 8:Lb, :], kvst[si][bb * 8:(bb + 1) * 8, 0:256],
                              reads=[("kvst", si)])
                        k.dma("sp", io["swv"][g][bb, Lb - 8:Lb, :], kvst[si][bb * 8:(bb + 1) * 8, 256:512],
                              reads=[("kvst", si)])
                    k.copy("pool", ks_bf[:], kvst[si][0:32, 0:256], reads=[("kvst", si)], writes=["ks_bf"])
                    k.copy("pool", vs_bf[:], kvst[si][0:32, 256:512], reads=[("kvst", si)], writes=["vs_bf"])
                    psb = k.ps[:, 7, :].bitcast(BF16)
                    for pair in range(2):
                        k.tr(psb[:, pair * 128:pair * 128 + 32], ks_bf[0:32, pair * 128:(pair + 1) * 128],
                             k.ident[0:32, 0:32], reads=["ks_bf", "ident"], writes=[("ps", 7)])
                    k.copy("dve", KsT[:], psb[:, 0:256].rearrange("p (a m) -> p a m", a=2)[:, :, 0:32],
                           reads=[("ps", 7)], writes=["KsT"])
            while qunits:
                q_unit(*qunits.pop(0))
            if msub < 2:
                return
            pend = None
            for blk in range(16):
                r, b_ = blk // (16 // d), blk % (16 // d)
                qpos = r * L + 128 * b_
                t0 = r + 128 * b_ * d

                def qap(h, qpos=qpos):
                    return QT[(h % 2) * 64:(h % 2) * 64 + 64, h // 2, qpos:qpos + 128]

                def mk_kt(pos):
                    return lambda h: KT[(h % 2) * 64:(h % 2) * 64 + 64, h // 2, pos:pos + 128]
                kbs = [(128, mk_kt(qpos), Vp[:, blk], (lambda par: bias[:, par, 0, :, :]), [("KT", blk), ("Vp", blk)])]
                if b_ > 0:
                    kbs.append((128, mk_kt(qpos - 128), Vp[:, blk - 1], (lambda par: bias[:, par, 1, :, :]),
                                [("KT", blk - 1), ("Vp", blk - 1)]))
                dst = NZ[:, :, :, sl(t0, 128, d)]
                qk = [("QT", oc, ci) for oc in range(2) for ci in range(4)]
                ctx_new = attn_block(g == 0, qap, 128, kbs, dst, qk, None)
                if pend is not None:
                    attn_B(pend)
                    if shift_jobs:
                        dd, ss_ = shift_jobs.pop(0)
                        k.dma("sp", dd, ss_, reads=[("pace", k._pace)])
                pend = ctx_new
            if pend is not None:
                attn_B(pend)
                pend = None
            if msub < 3:
                continue
            P.barrier()
            for bb in range(4):
                i_ = k.rot("cset", 2)
                rowsel = [[d * 256, 128], [256, C], [1, 256]]
                srck = bass.AP(tensor=io["cwk"][g].tensor, offset=io["cwk"][g][bb].offset, ap=rowsel)
                srcv = bass.AP(tensor=io["cwv"][g].tensor, offset=io["cwv"][g][bb].offset, ap=rowsel)
                k.dma("pool", kc8[i_][:, 0:C, :], srck, writes=[("kc8", i_)])
                k.dma("pool", vc8[i_][:, 0:C, :], srcv, writes=[("vc8", i_)])
                d2 = Vpc8[i_][:, 0:C].rearrange("p c (a b) x -> p c a b x", b=2)
                s2 = vc8[i_][:, 0:C, :].rearrange("p c (a b x) -> p c a b x", a=2, b=2)
                if bb < 2:
                    k.memset("pool", Vpc8[i_][:, 0:C], 0.0, writes=[("Vpc8", i_)])
                for par in range(2):
                    k.copy("pool", d2[:, :, :, par, par * 64:(par + 1) * 64], s2[:, :, :, par, :],
                           reads=[("vc8", i_)], writes=[("Vpc8", i_)])
                psb2 = k.ps[:, 2:4, :].rearrange("p b x -> p (b x)").bitcast(BF16)
                for r in range(C):
                    for pair in range(2):
                        o_ = (r * 2 + pair) * 128
                        k.tr(psb2[:, o_:o_ + 128], kc8[i_][:, r, pair * 128:(pair + 1) * 128], k.ident[:],
                             reads=[("kc8", i_), "ident"], writes=[("ps", 2 + o_ // 1024)])
                k.copy("dve", KcT8[:, 0:C], psb2[:, 0:C * 256].rearrange("p (c a m) -> p c a m", c=C, a=2),
                       reads=[("ps", 2), ("ps", 3)], writes=["KcT8"])
                k.mm(k.ps[0:8, 3, 256:512], k.ident[0:32, bb * 8:(bb + 1) * 8], vs_bf[0:32, :], True, True,
                     reads=["ident", "vs_bf"], writes=[("ps", 3)])
                pad_copy("act", Vpn[i_], k.ps[0:8, 3, 256:512], 8, [("ps", 3)], [("Vpn", i_)])

                def qap(h, bb=bb):
                    return QT[(h % 2) * 64:(h % 2) * 64 + 64, h // 2, S + bb * 8:S + bb * 8 + 8]
                kbs = []
                for r in range(C):
                    kbs.append((128, (lambda h, r=r: KcT8[(h % 2) * 64:(h % 2) * 64 + 64, r, h // 2, :]), Vpc8[i_][:, r],
                                None, ["KcT8", ("Vpc8", i_)]))
                kbs.append((8, (lambda h, bb=bb: KsT[(h % 2) * 64:(h % 2) * 64 + 64, h // 2, bb * 8:(bb + 1) * 8]),
                            Vpn[i_], None, ["KsT", ("Vpn", i_)]))
                dst = NZ[:, :, :, S + bb * 8:S + bb * 8 + 8]
                ctx_new = attn_block(g == 0, qap, 8, kbs, dst, [("QT", oc, 4) for oc in range(2)], None,
                                     fullbias=(lambda par, NB=NB: sbias[:, par, 0:NB, :, :].rearrange("p b a q -> p (b a q)")))
                if pend is not None:
                    attn_B(pend)
                pend = ctx_new
            if pend is not None:
                attn_B(pend)
                pend = None
            P.barrier()
            if g < 2:
                k.memset("pool", Vp[:], 0.0, writes=[("Vp", b) for b in range(16)])

        while shift_jobs:
            dd, ss_ = shift_jobs.pop(0)
            k.dma("sp", dd, ss_, reads=[("pace", k._pace)])
        P.barrier()
        ar.reset(mark1)
        P.op("dve", lambda e: e.reciprocal(out=NZ[:, 1, :, :], in_=NZ[:, 1, :, :]), reads=["NZ"], writes=["NZ"])
        k.tt("dve", oT[:], NZ[:, 0, :, :], NZ[:, 1, :, :], ALU.mult, reads=["NZ"],
             writes=[("oT", a, ci) for a in range(2) for ci in range(5)])
        k.mix_tail(hT, oT, mark0, msub)

    def mix_tail(self, hT, oT, mark0, msub=9):
        k = self
        P = k.P
        W = k.W
        io = k.io
        ar = k.ar
        win = W["w_in"][0]
        cT = ar.alloc([128, 4, NT], BF16)
        mark = ar.off
        uT = ar.alloc([128, 4, 32 + S], BF16)
        usT = ar.alloc([128, 4, 4, 38], BF16)
        ulast = ar.alloc([128, 4, 32], F32)
        unew = ar.alloc([128, 4, 32], F32)
        markb = ar.off
        k.wslab = [ar.alloc([128, 8, 256], BF16) for _ in range(2)]
        wsl2 = [ar.alloc([128, 8, 256], BF16) for _ in range(2)]
        sgb = [ar.alloc([128, 512], F32) for _ in range(2)]
        stg = ar.alloc([128, 512], F32)
        ostg = ar.alloc([32, 512], F32)

        k.memset("pool", uT[:, :, 0:32], 0.0, writes=["uTpad"])
        k.dma("sp", stg[0:120, :], io["stc"].rearrange("b r c -> (b r) c"), writes=["stg"])
        for oc in range(4):
            k.tr(k.ps[:, 7, oc * 128:oc * 128 + 120], stg[0:120, oc * 128:(oc + 1) * 128], k.identf[0:120, 0:120],
                 reads=["stg", "identf"], writes=[("ps", 7)])
        k.copy("act", usT[:, :, :, 0:30], k.ps[:, 7, :].rearrange("p (oc x) -> p oc x", oc=4)[:, :, 0:120]
               .rearrange("p oc (b r) -> p oc b r", b=4), reads=[("ps", 7)], writes=["usT"])

        for sl_ in range(2):
            wi = k.rot("wq", 2)
            k.dma("pool", k.wslab[wi][:], win[:, 2304 + sl_ * 256:2304 + (sl_ + 1) * 256].rearrange("(kc p) f -> p kc f", p=128),
                  writes=[("wslab", wi)])
            k.dma("pool", wsl2[wi][:], win[:, 2816 + sl_ * 256:2816 + (sl_ + 1) * 256].rearrange("(kc p) f -> p kc f", p=128),
                  writes=[("wsl2", wi)])
            for j in range(2):
                oc = sl_ * 2 + j
                for ci in range(5):
                    cs, n = CHUNKS[ci]
                    ba, bb_ = k.rot("bua", 2), 2 + k.rot("bub", 2)
                    for kc in range(8):
                        k.mm(k.ps[:, ba, 0:n], k.wslab[wi][:, kc, j * 128:(j + 1) * 128], hT[:, kc, cs:cs + n], kc == 0, kc == 7,
                             reads=[("wslab", wi), ("hT", kc, ci)], writes=[("ps", ba)])
                    for kc in range(8):
                        k.mm(k.ps[:, bb_, 0:n], wsl2[wi][:, kc, j * 128:(j + 1) * 128], hT[:, kc, cs:cs + n], kc == 0, kc == 7,
                             reads=[("wsl2", wi), ("hT", kc, ci)], writes=[("ps", bb_)])
                    si = k.rot("sgb", 2)
                    k.act(sgb[si][:, 0:n], k.ps[:, bb_, 0:n], AF.Sigmoid, reads=[("ps", bb_)], writes=[("sgb", si)])
                    if ci < 4:
                        k.tt("dve", uT[:, oc, 32 + cs:32 + cs + n], k.ps[:, ba, 0:n], sgb[si][:, 0:n], ALU.mult,
                             reads=[("ps", ba), ("sgb", si)], writes=[("uT", oc, ci)])
                        if ci == 3:
                            k.tt("dve", ulast[:, oc, :], k.ps[:, ba, n - 32:n], sgb[si][:, n - 32:n], ALU.mult,
                                 reads=[("ps", ba), ("sgb", si)], writes=["ulast"])
                    else:
                        k.tt("dve", unew[:, oc, :], k.ps[:, ba, 0:32], sgb[si][:, 0:32], ALU.mult,
                             reads=[("ps", ba), ("sgb", si)], writes=["unew"])
                        k.copy("dve", usT[:, oc, :, 30:38], unew[:, oc, :].rearrange("p (b t) -> p b t", b=4),
                               reads=["unew", "usT"], writes=["usT"])
        for which, srcu in ((0, ulast), (1, unew)):
            for oc in range(4):
                k.tr(k.ps[0:32, 7, oc * 128:(oc + 1) * 128], srcu[:, oc, :], k.identf[:],
                     reads=["ulast", "unew", "identf"], writes=[("ps", 7)])
            k.copy("act", ostg[:], k.ps[0:32, 7, :], reads=[("ps", 7)], writes=["ostg"])
            if which == 0:
                k.dma("sp", io["pconv"][:, :], ostg[2:32, :], reads=["ostg"])
            else:
                for bq in range(4):
                    k.dma("sp", io["sconv"][bq, 22:30, :], ostg[bq * 8:(bq + 1) * 8, :], reads=["ostg"])

        if msub < 6:
            return
        P.barrier()
        ar.reset(mark0)
        dg = ar.alloc([128, 4, 31, 128], BF16)
        ar.reset(markb)
        wT = ar.alloc([128, 4, 31], F32)
        cvec = ar.alloc([128, 3, 4], F32)
        onesf = k.onesf
        ybuf = ar.alloc([128, 4, 512], F32)
        ysqf = ar.alloc([128, 4, 512], F32)
        st = [ar.alloc([128, 16], F32)]
        tb = [ar.alloc([128, 512], F32) for _ in range(2)]
        k.dma("sp", ybuf[0:31, 0, :], W["conv_dw_w"][0], writes=[("ybuf", 0)])
        for oc in range(4):
            k.tr(k.ps[:, 7, oc * 32:oc * 32 + 31], ybuf[0:31, 0, oc * 128:(oc + 1) * 128], k.identf[0:31, 0:31],
                 reads=[("ybuf", 0), "identf"], writes=[("ps", 7)])
        k.copy("act", wT[:], k.ps[:, 7, 0:128].rearrange("p (oc x) -> p oc x", oc=4)[:, :, 0:31], reads=[("ps", 7)],
               writes=["wT"])
        for i, nm in enumerate(("conv_dw_b", "conv_ln_g", "conv_ln_b")):
            k.dma("sp", cvec[:, i, :], W[nm][0].rearrange("(oc p) -> p oc", p=128), writes=[("cvec", i)],
                  allow_slow_non_contiguous=True)
        for oc in range(4):
            k.tt("dve" if oc % 2 == 0 else "pool", dg[:, oc, :, :],
                 k.identf[:].unsqueeze(1).to_broadcast([128, 31, 128]),
                 wT[:, oc, :].unsqueeze(2).to_broadcast([128, 31, 128]), ALU.mult,
                 reads=["identf", "wT"], writes=[("dg", oc)])
        def conv_chunk(ci):
            cs, n = CHUNKS[ci]
            for oc in range(4):
                b = 0 + k.rot("bcv", 2)
                for kk_ in range(31):
                    if ci < 4:
                        rhs = uT[:, oc, cs + kk_ + 2:cs + kk_ + 2 + n]
                        rk = [("uT", oc, c2) for c2 in range(4)] + ["uTpad"]
                    else:
                        rhs = usT[:, oc, :, kk_:kk_ + 8]
                        rk = ["usT"]
                    outp = k.ps[:, b, 0:n] if ci < 4 else k.ps[:, b, 0:32].rearrange("p (b t) -> p b t", b=4)
                    k.mm(outp, dg[:, oc, kk_, :], rhs, kk_ == 0, kk_ == 30, reads=rk + [("dg", oc)], writes=[("ps", b)])
                k.ts("dve", ybuf[:, oc, 0:n], k.ps[:, b, 0:n], cvec[:, 0, oc:oc + 1], None, ALU.add,
                     reads=[("ps", b), ("cvec", 0)], writes=[("ybuf", oc)])
                k.act(ysqf[:, oc, 0:n], ybuf[:, oc, 0:n], AF.Square, reads=[("ybuf", oc)], writes=[("ysqf", oc)])
            ntl = (n + 127) // 128
            R = min(n, 128)
            for which in range(2):
                for c in range(ntl):
                    for o2 in range(4):
                        lhs = ybuf[:, o2, c * 128:c * 128 + R] if which == 0 else ysqf[:, o2, c * 128:c * 128 + R]
                        k.mm(k.ps[0:R, 2, which * 4 + c:which * 4 + c + 1], lhs, k.onesf[:, 0:1], o2 == 0, o2 == 3,
                             reads=["onesf", ("ybuf", o2), ("ysqf", o2)], writes=[("ps", 2)])
            cs_ = st[0]
            k.ts("dve", cs_[0:R, 0:8], k.ps[0:R, 2, 0:8], 1.0 / 512, None, ALU.mult, reads=[("ps", 2)], writes=["cst"])
            k.tt("dve", cs_[0:R, 8:8 + ntl], cs_[0:R, 0:ntl], cs_[0:R, 0:ntl], ALU.mult, reads=["cst"], writes=["cst"])
            k.tt("dve", cs_[0:R, 4:4 + ntl], cs_[0:R, 4:4 + ntl], cs_[0:R, 8:8 + ntl], ALU.subtract, reads=["cst"], writes=["cst"])
            k.ts("dve", cs_[0:R, 4:4 + ntl], cs_[0:R, 4:4 + ntl], EPS, None, ALU.add, reads=["cst"], writes=["cst"])
            k.tt("pool", cs_[0:R, 4:4 + ntl], cs_[0:R, 4:4 + ntl], k.mhalf[0:R, 0:ntl], ALU.pow, reads=["cst", "mhalf"], writes=["cst"])
            k.stt(cs_[0:R, 8:8 + ntl], cs_[0:R, 0:ntl], -1.0, cs_[0:R, 4:4 + ntl], ALU.mult, ALU.mult, reads=["cst"], writes=["cst"])
            for which, off in ((0, 4), (1, 8)):
                for c in range(ntl):
                    k.ts("dve", k.rdg[which][0:R, c, 0:R], k.identf[0:R, 0:R], cs_[0:R, off + c:off + c + 1], None, ALU.mult,
                         reads=["identf", "cst"], writes=[("rdg", which, c)])
                for c in range(ntl):
                    k.mm(k.ps[:, 2 + which, c * 128:c * 128 + R], k.onesf[0:R, :], k.rdg[which][0:R, c, 0:R], True, True,
                         reads=["onesf", ("rdg", which, c)], writes=[("ps", 2 + which)])
            rs = k.ps[:, 2, :]
            nb = k.ps[:, 3, :]
            for oc in range(4):
                ti = k.rot("tb", 2)
                k.tt("dve", tb[ti][:, 0:n], ybuf[:, oc, 0:n], rs[:, 0:n], ALU.mult, reads=[("ybuf", oc), ("ps", 2)],
                     writes=[("tb", ti)])
                k.tt("dve", tb[ti][:, 0:n], tb[ti][:, 0:n], nb[:, 0:n], ALU.add, reads=[("tb", ti), ("ps", 3)],
                     writes=[("tb", ti)])
                k.act(cT[:, oc, cs:cs + n], tb[ti][:, 0:n], AF.Silu, reads=[("tb", ti), ("cvec", 1), ("cvec", 2)],
                      writes=[("cT", oc, ci)], scale=cvec[:, 1, oc:oc + 1], bias=cvec[:, 2, oc:oc + 1])
        for ci in range(5):
            conv_chunk(ci)
        P.barrier()

        if msub < 7:
            return
        ar.reset(mark0)
        merged = ar.alloc([128, 8, NT], BF16)
        ar.reset(mark)
        k.wslab = [ar.alloc([128, 8, 256], BF16) for _ in range(2)]
        wsl2 = [ar.alloc([128, 8, 256], BF16) for _ in range(2)]
        wap = ar.alloc([128, 2, D], BF16)
        wcp = ar.alloc([128, 4, D], BF16)
        sga = [ar.alloc([128, 512], F32) for _ in range(2)]
        sgb2 = [ar.alloc([128, 512], F32) for _ in range(2)]
        m1 = [ar.alloc([128, 512], F32) for _ in range(2)]
        m2 = [ar.alloc([128, 512], F32) for _ in range(2)]
        k.dma("pool", wap[:], W["w_attn_proj"][0].rearrange("(kc p) f -> p kc f", p=128), writes=["wap"])
        k.dma("pool", wcp[:], W["w_conv_proj"][0].rearrange("(kc p) f -> p kc f", p=128), writes=["wcp"])
        for sl_ in range(4):
            wi = k.rot("wq", 2)
            k.dma("pool", k.wslab[wi][:], win[:, 3328 + sl_ * 256:3328 + (sl_ + 1) * 256].rearrange("(kc p) f -> p kc f", p=128),
                  writes=[("wslab", wi)])
            k.dma("pool", wsl2[wi][:], win[:, 4352 + sl_ * 256:4352 + (sl_ + 1) * 256].rearrange("(kc p) f -> p kc f", p=128),
                  writes=[("wsl2", wi)])
            for j in range(2):
                oc = sl_ * 2 + j
                for ci in range(5):
                    cs, n = CHUNKS[ci]
                    b0, b1, b2, b3 = k.rot("g0", 2), 2 + k.rot("g1", 2), 4 + k.rot("g2", 2), 6 + k.rot("g3", 2)
                    for kc in range(8):
                        k.mm(k.ps[:, b0, 0:n], k.wslab[wi][:, kc, j * 128:(j + 1) * 128], hT[:, kc, cs:cs + n], kc == 0, kc == 7,
                             reads=[("wslab", wi), ("hT", kc, ci)], writes=[("ps", b0)])
                    for kc in range(2):
                        k.mm(k.ps[:, b1, 0:n], wap[:, kc, oc * 128:(oc + 1) * 128], oT[:, kc, cs:cs + n], kc == 0, kc == 1,
                             reads=["wap", ("oT", kc, ci)], writes=[("ps", b1)])
                    for kc in range(8):
                        k.mm(k.ps[:, b2, 0:n], wsl2[wi][:, kc, j * 128:(j + 1) * 128], hT[:, kc, cs:cs + n], kc == 0, kc == 7,
                             reads=[("wsl2", wi), ("hT", kc, ci)], writes=[("ps", b2)])
                    for kc in range(4):
                        k.mm(k.ps[:, b3, 0:n], wcp[:, kc, oc * 128:(oc + 1) * 128], cT[:, kc, cs:cs + n], kc == 0, kc == 3,
                             reads=["wcp", ("cT", kc, ci)], writes=[("ps", b3)])
                    i = k.rot("sga", 2)
                    k.act(sga[i][:, 0:n], k.ps[:, b0, 0:n], AF.Sigmoid, reads=[("ps", b0)], writes=[("sga", i)])
                    k.act(sgb2[i][:, 0:n], k.ps[:, b2, 0:n], AF.Sigmoid, reads=[("ps", b2)], writes=[("sgb2", i)])
                    k.tt("dve", m1[i][:, 0:n], k.ps[:, b1, 0:n], sga[i][:, 0:n], ALU.mult, reads=[("ps", b1), ("sga", i)],
                         writes=[("m1", i)])
                    k.tt("dve", m2[i][:, 0:n], k.ps[:, b3, 0:n], sgb2[i][:, 0:n], ALU.mult, reads=[("ps", b3), ("sgb2", i)],
                         writes=[("m2", i)])
                    k.tt("dve", merged[:, oc, cs:cs + n], m1[i][:, 0:n], m2[i][:, 0:n], ALU.add,
                         reads=[("m1", i), ("m2", i)], writes=[("mg", oc, ci)])

        def evac_o(oc, ci, cs, n, b):
            k.tt("dve", k.xT[:, oc, cs:cs + n], k.ps[:, b, 0:n], k.xT[:, oc, cs:cs + n], ALU.add,
                 reads=[("ps", b), ("xT", oc, ci)], writes=[("xT", oc, ci)])
        k.projB(W["w_o"][0], 0, D, merged, "mg", 8, range(5), evac_o, "wq")

    def projB(self, w2d, col0, ncols, rhs, rhs_key, nk, cis, evac, wname, banks=(0, 1, 2, 3)):
        k = self
        for s0 in range(0, ncols, 256):
            wc = min(256, ncols - s0)
            wi = k.rot(wname, 2)
            wt = k.wslab[wi]
            k.dma("pool", wt[:, 0:nk, 0:wc],
                  w2d[:, col0 + s0:col0 + s0 + wc].rearrange("(kc p) f -> p kc f", p=128),
                  writes=[("wslab", wi)])
            for j in range(wc // 128):
                oc = s0 // 128 + j
                for ci in cis:
                    cs, n = CHUNKS[ci]
                    b = banks[k.rot("pb%s" % (banks,), len(banks))]
                    for kc in range(nk):
                        k.mm(k.ps[:, b, 0:n], wt[:, kc, j * 128:(j + 1) * 128], rhs[:, kc, cs:cs + n],
                             kc == 0, kc == nk - 1, reads=[("wslab", wi), (rhs_key, kc, ci)], writes=[("ps", b)])
                    evac(oc, ci, cs, n, b)

    def xattn(self):
        k = self
        W = k.W
        P = k.P
        ar = k.ar
        ar.reset()
        mk_tok = [ar.alloc([128, 2, D], BF16) for _ in range(2)]
        mv_tok = [ar.alloc([128, 2, D], BF16) for _ in range(2)]
        mkT = [ar.alloc([128, 8, 256], BF16) for _ in range(2)]
        hT = ar.alloc([128, 8, NT], BF16)
        mark = ar.off
        sq = [ar.alloc([128, 8, 512], BF16)]
        sq2 = [ar.alloc([128, 8, 512], BF16)]
        rstd = None
        k.norm(NORM_IDX["xattn_norm"], hT, sq2, rstd)
        wkv = [ar.alloc([128, 8, 512], BF16) for _ in range(2)]
        memT = ar.alloc([128, 8, 256], F32)
        mhT = ar.alloc([128, 8, 256], BF16)
        mstg = [ar.alloc([128, D], F32) for _ in range(2)]
        ostg = [ar.alloc([128, 512], F32) for _ in range(2)]

        import os
        xsub = 9
        if xsub < -2:
            return
        for mt in range(2):
            k.dma("sp", mstg[mt][:], k.io["mem"][mt * 128:(mt + 1) * 128, :], writes=[("mstg", mt)])
            bp = 6 if mt == 0 else 4
            for kc in range(8):
                k.tr(k.ps[:, bp + kc // 4, (kc % 4) * 128:(kc % 4 + 1) * 128], mstg[mt][:, kc * 128:(kc + 1) * 128],
                     k.identf[:], reads=[("mstg", mt), "identf"], writes=[("ps", bp + kc // 4)])
            k.copy("act", memT[:, :, mt * 128:(mt + 1) * 128],
                   k.ps[:, bp:bp + 2, :].rearrange("p b (k r) -> p (b k) r", k=4),
                   reads=[("ps", bp), ("ps", bp + 1)], writes=[("memT", kc, 0) for kc in range(8)])
        if xsub < -1:
            return
        k.norm(NORM_IDX["mem_norm"], mhT, sq, rstd, chunks=[(0, 256)], src=memT, keyp="mhT", srckey="memT")
        if xsub < 0:
            return
        for cc in range(4):
            wi = k.rot("wkv", 2)
            k.dma("pool", wkv[wi][:], W["w_xkv"][0, :, cc * 512:(cc + 1) * 512].rearrange("(kc p) f -> p kc f", p=128),
                  writes=[("wkv", wi)])
            for mt in range(2):
                b = k.rot("pbkv", 2)
                for kc in range(8):
                    k.mm(k.ps[:, b, :], mhT[:, kc, mt * 128:(mt + 1) * 128], wkv[wi][:, kc, :], kc == 0, kc == 7,
                         reads=[("wkv", wi), ("mhT", kc, 0)], writes=[("ps", b)])
                xv = ""
                oi = k.rot("ostg", 2)
                if "a" not in xv:
                    k.copy("act", ostg[oi][:], k.ps[:, b, :], reads=[("ps", b)], writes=[("ostg", oi)])
                dst = (k.io["pmk"] if cc < 2 else k.io["pmv"])[mt * 128:(mt + 1) * 128, (cc % 2) * 512:(cc % 2 + 1) * 512]
                if "d" not in xv:
                    k.dma("sp", dst, ostg[oi][:], reads=[("ostg", oi)])
                tgt = (mk_tok if cc < 2 else mv_tok)[0]
                if "v" not in xv:
                    k.copy("dve", tgt[:, mt, (cc % 2) * 512:(cc % 2 + 1) * 512], ostg[oi][:], reads=[("ostg", oi)],
                           writes=[("mk_tok" if cc < 2 else "mv_tok", 0, mt, cc % 2)])

        def make_mkT(slot, mkt, keyname):
            for mt in range(2):
                b = 6 + k.rot("pbt", 2)
                psb = k.ps[:, b, :].bitcast(BF16)
                for c8 in range(8):
                    k.tr(psb[:, c8 * 128:(c8 + 1) * 128], mkt[:, mt, c8 * 128:(c8 + 1) * 128], k.ident[:],
                         reads=[(keyname, slot, mt, c8 // 4), "ident"], writes=[("ps", b)])
                k.copy("act", mkT[slot][:, :, mt * 128:(mt + 1) * 128], psb.rearrange("p (c m) -> p c m", c=8),
                       reads=[("ps", b)], writes=[("mkT", slot, mt)])
        import os
        xsub = 9
        if xsub < 1:
            return
        make_mkT(0, mk_tok[0], "mk_tok")
        if xsub < 2:
            return
        P.barrier()
        ar.reset(mark)
        QT = ar.alloc([128, 8, NT], BF16)
        k.wslab = [ar.alloc([128, 8, 256], BF16) for _ in range(2)]
        PT = [ar.alloc([128, 512], BF16) for _ in range(4)]
        rz = [ar.alloc([128, 512], F32) for _ in range(2)]

        def evac_q(oc, ci, cs, n, b):
            k.act(QT[:, oc, cs:cs + n], k.ps[:, b, 0:n], AF.Copy, reads=[("ps", b)], writes=[("QT", oc, ci)],
                  scale=1.0 / 16.0)
        k.projB(W["w_xq"][0], 0, D, hT, "hT", 8, range(5), evac_q, "wq")

        if xsub < 3:
            return
        oT = hT

        def attend(slot, cs, n, ci, qkeys_ci):
            for h in range(4):
                it = k.rot("att_it", 2)
                sb0 = 2
                ob = (0, 1) if it == 0 else (4, 5)
                zb = 6 + it
                pts = []
                for mt in range(2):
                    b = sb0 + mt
                    for dc in range(2):
                        k.mm(k.ps[:, b, 0:n], mkT[slot][:, 2 * h + dc, mt * 128:(mt + 1) * 128],
                             QT[:, 2 * h + dc, cs:cs + n], dc == 0, dc == 1,
                             reads=[("mkT", slot, mt), ("QT", 2 * h + dc, qkeys_ci)], writes=[("ps", b)])
                    pi = k.rot("PT", 4)
                    k.act(PT[pi][:, 0:n], k.ps[:, b, 0:n], AF.Exp, reads=[("ps", b)], writes=[("PT", pi)])
                    pts.append(pi)
                for dc in range(2):
                    for mt in range(2):
                        k.mm(k.ps[:, ob[dc], 0:n], mv_tok[slot][:, mt, (2 * h + dc) * 128:(2 * h + dc + 1) * 128],
                             PT[pts[mt]][:, 0:n], mt == 0, mt == 1,
                             reads=[("mv_tok", slot, mt, (2 * h + dc) // 4), ("PT", pts[mt])], writes=[("ps", ob[dc])])
                for mt in range(2):
                    k.mm(k.ps[:, zb, 0:n], k.ones[:], PT[pts[mt]][:, 0:n], mt == 0, mt == 1,
                         reads=["ones", ("PT", pts[mt])], writes=[("ps", zb)])
                ri = k.rot("rz", 2)
                P.op("dve", lambda e, ri=ri, zb=zb: e.reciprocal(out=rz[ri][:, 0:n], in_=k.ps[:, zb, 0:n]),
                     reads=[("ps", zb)], writes=[("rz", ri)])
                for dc in range(2):
                    k.tt("dve", oT[:, 2 * h + dc, cs:cs + n], k.ps[:, ob[dc], 0:n], rz[ri][:, 0:n], ALU.mult,
                         reads=[("ps", ob[dc]), ("rz", ri)], writes=[("hT", 2 * h + dc, ci)])

        for ci in range(4):
            cs, n = CHUNKS[ci]
            attend(0, cs, n, ci, ci)
        if xsub < 4:
            return
        for bb in range(4):
            slot = 1
            for mt in range(2):
                k.dma("pool", mk_tok[1][:, mt, :], k.io["cmk"][bb, mt * 128:(mt + 1) * 128, :],
                      writes=[("mk_tok", 1, mt, 0), ("mk_tok", 1, mt, 1)])
                k.dma("pool", mv_tok[1][:, mt, :], k.io["cmv"][bb, mt * 128:(mt + 1) * 128, :],
                      writes=[("mv_tok", 1, mt, 0), ("mv_tok", 1, mt, 1)])
            make_mkT(1, mk_tok[1], "mk_tok")
            attend(1, S + bb * 8, 8, 4, 4)

        if xsub < 5:
            return
        def evac_o(oc, ci, cs, n, b):
            k.tt("dve", k.xT[:, oc, cs:cs + n], k.ps[:, b, 0:n], k.xT[:, oc, cs:cs + n], ALU.add,
                 reads=[("ps", b), ("xT", oc, ci)], writes=[("xT", oc, ci)])
        k.projB(W["w_xo"][0], 0, D, oT, "hT", 8, range(5), evac_o, "wq")

    def final_norm(self):
        k = self
        ar = k.ar
        ar.reset()
        gbc = ar.alloc([128, D], F32)
        junk = [ar.alloc([128, D], BF16) for _ in range(2)]
        ost = [ar.alloc([128, D], F32) for _ in range(3)]
        stat = ar.alloc([128, 17, 2], F32)
        k.dma("sp", gbc[:], k.W["final_norm"].partition_broadcast(128), writes=["gbc"])
        for t in range(17):
            R = 128 if t < 16 else 32
            bp = 6 if t % 2 == 0 else 4
            ci = min(t // 4, 4)
            for kc in range(8):
                k.tr(k.ps[0:R, bp + kc // 4, (kc % 4) * 128:(kc % 4 + 1) * 128],
                     k.xT[:, kc, t * 128:t * 128 + R], k.identf[:, :],
                     reads=[("xT", kc, ci), "identf"], writes=[("ps", bp + kc // 4)])
            xt = k.ps[0:R, bp:bp + 2, :].rearrange("p b c -> p (b c)")
            ji = k.rot("fjunk", 2)
            k.act(junk[ji][0:R, :], xt, AF.Square, reads=[("ps", bp), ("ps", bp + 1)], writes=[("fstat", t, 0)],
                  accum_out=stat[0:R, t, 0:1])
            k.ts("dve", stat[0:R, t, 1:2], stat[0:R, t, 0:1], 1.0 / D, EPS, ALU.mult, ALU.add,
                 reads=[("fstat", t, 0)], writes=[("fstat", t, 1)])
            k.tt("pool", stat[0:R, t, 1:2], stat[0:R, t, 1:2], k.mhalf[0:R, 0:1], ALU.pow,
                 reads=[("fstat", t, 1), "mhalf"], writes=[("fstat", t, 1)])
            oi = k.rot("ost", 3)
            k.stt(ost[oi][0:R, :], xt, stat[0:R, t, 1:2], gbc[0:R, :], ALU.mult, ALU.mult,
                  reads=[("ps", bp), ("ps", bp + 1), ("fstat", t, 1), "gbc"], writes=[("ost", oi)])
            dst = k.io["yp"][t * 128:(t + 1) * 128, :] if t < 16 else k.io["ys"][:, :]
            k.dma("sp", dst, ost[oi][0:R, :], reads=[("ost", oi)])

    def norm_chunk(self, gi, yv, sq, rstd, ci, cs, n, yi):
        k = self
        rb, rkey = k.rstd_bc(k.xT, "xT", ci, cs, n, sq, banks=(2, 3))
        for kc in range(8):
            k.stt(yv[:, kc, 0:n], k.xT[:, kc, cs:cs + n], k.gcol[:, gi, kc:kc + 1], rb,
                  ALU.mult, ALU.mult, reads=[("xT", kc, ci), rkey, ("gcol", gi)], writes=[("yT", yi)])


_CACHE = {}


def _get_nc(stage):
    if stage not in _CACHE:
        b = Builder(stage)
        _CACHE[stage] = b.build()
    return _CACHE[stage]


def _rel_bucket_np(n):
    n = np.maximum(n, 0)
    nf = np.maximum(n, 1).astype(np.float32)
    large = 16 + ((np.log(nf / np.float32(16)) / np.float32(np.log(128.0))) * np.float32(16)).astype(np.int32)
    return np.where(n < 16, n, np.minimum(large, 31))


def _ebuck():
    E = np.zeros((3, 33, 384), np.float32)
    for g, (w, d) in enumerate(GROUPS):
        for u in range(383):
            delta = u - 127
            if 0 <= delta <= 128:
                E[g, int(_rel_bucket_np(np.array(delta * d))), u] = 1.0
            else:
                E[g, 32, u] = NEG
        E[g, 32, 383] = NEG
    return E


def kernel(_stage="full", **inputs):
    nc = _get_nc(_stage)
    eb = _ebuck()
    _cache_names = ("cache_win0_k", "cache_win0_v", "cache_win1_k", "cache_win1_v", "cache_win2_k", "cache_win2_v")
    assert all(n in inputs for n in _cache_names)
    f32 = lambda a: np.ascontiguousarray(np.asarray(a, dtype=np.float32))
    in_maps = []
    for c in range(NCORES):
        m = {"xp": f32(inputs["x_prompt"][c]),
             "xs": f32(inputs["x_sample"][4 * c:4 * c + 4]).reshape(NS_TOK, D),
             "mem": f32(inputs["mem_prompt"][c]),
             "stc": f32(inputs["state_conv"][0, 4 * c:4 * c + 4]),
             "cmk": f32(inputs["cache_mem_k"][0, 4 * c:4 * c + 4]).reshape(4, 256, D),
             "cmv": f32(inputs["cache_mem_v"][0, 4 * c:4 * c + 4]).reshape(4, 256, D)}
        for g in range(3):
            m["cw%dk" % g] = f32(inputs["cache_win%d_k" % g][0, 4 * c:4 * c + 4]).reshape(4, GROUPS[g][0], 256)
            m["cw%dv" % g] = f32(inputs["cache_win%d_v" % g][0, 4 * c:4 * c + 4]).reshape(4, GROUPS[g][0], 256)
        for n in WNAMES:
            m[n] = f32(inputs[n])
        m["ebuck"] = eb
        in_maps.append(m)
    res = run_bass_kernel_spmd(nc, in_maps, core_ids=list(range(NCORES)))
    R = res.results
    cat = lambda name: np.stack([np.asarray(R[c][name]) for c in range(NCORES)], axis=0)
    y_prompt = cat("yp")
    y_sample = cat("ys").reshape(32, 8, D)
    outs = [y_prompt, y_sample]
    for g in range(3):
        Lg = GROUPS[g][0]
        outs.append(cat("pw%dk" % g).reshape(1, 8, Lg, 4, 64))
        outs.append(cat("pw%dv" % g).reshape(1, 8, Lg, 4, 64))
    outs.append(cat("pconv").reshape(1, 8, 30, 512))
    outs.append(cat("pmk").reshape(1, 8, 256, 4, 256))
    outs.append(cat("pmv").reshape(1, 8, 256, 4, 256))
    for g in range(3):
        Lg = GROUPS[g][0]
        outs.append(cat("sw%dk" % g).reshape(1, 32, Lg, 4, 64))
        outs.append(cat("sw%dv" % g).reshape(1, 32, Lg, 4, 64))
    outs.append(cat("sconv").reshape(1, 32, 30, 512))
    return tuple(outs)
```

```python
import numpy as np
import concourse.bass as bass
import concourse.mybir as mybir
from concourse.bass_utils import run_bass_kernel_spmd

F32 = mybir.dt.float32
BF16 = mybir.dt.bfloat16
AF = mybir.ActivationFunctionType
ALU = mybir.AluOpType

NCORES = 8
D = 1024
S = 2048
NS_TOK = 32
NT = S + NS_TOK
DFF = 2816
NF = DFF // 128
EPS = 1e-6
GROUPS = ((128, 1), (512, 4), (2048, 16))
IN_WIDTH = 5376
NEG = -30000.0

ENGS = ("pe", "act", "dve", "pool", "sp")
NDSEM = {"sp": 44, "pool": 12, "act": 4}


class _Op:
    __slots__ = ("eng", "fn", "deps", "is_dma", "signal", "ticket", "sem", "prev_dma", "tag")

    def __init__(self, eng, fn, is_dma, tag):
        self.eng = eng
        self.fn = fn
        self.is_dma = is_dma
        self.deps = []
        self.signal = False
        self.ticket = 0
        self.sem = None
        self.prev_dma = None
        self.tag = tag


class Prog:
    def __init__(self, nc):
        self.nc = nc
        self.ops = {e: [] for e in ENGS}
        self.state = {}
        self.bar = []

    def barrier(self):
        b = []
        for e in ENGS:
            comp = [o for o in self.ops[e] if not o.is_dma]
            if comp:
                b.append(comp[-1])
            if e in NDSEM:
                dm = [o for o in self.ops[e] if o.is_dma]
                b.extend(dm[-NDSEM[e]:])
        self.bar = b
        self.state = {}

    def op(self, eng, fn, reads=(), writes=(), dma=False, tag=None):
        o = _Op(eng, fn, dma, tag)
        deps = {}

        def add(d, kind):
            if d is None or d is o:
                return
            if d.is_dma or o.is_dma:
                deps[id(d)] = d
            elif d.eng == eng:
                if eng != "pe" and kind != "war":
                    deps[id(d)] = d
            else:
                deps[id(d)] = d

        for k in reads:
            st = self.state.get(k)
            if st is not None:
                add(st[0], "raw")
            else:
                for d in self.bar:
                    add(d, "raw")
        for k in writes:
            st = self.state.get(k)
            if st is None:
                for d in self.bar:
                    add(d, "raw")
            if st is not None:
                add(st[0], "waw")
                for r in st[1].values():
                    add(r, "war")
                for r in st[2]:
                    add(r, "war")
        o.deps = list(deps.values())
        for k in reads:
            st = self.state.setdefault(k, [None, {}, []])
            if dma:
                st[2].append(o)
            else:
                st[1][eng] = o
        for k in writes:
            self.state[k] = [o, {}, []]
        self.ops[eng].append(o)
        return o

    def emit(self):
        nc = self.nc
        for e in ENGS:
            for o in self.ops[e]:
                for d in o.deps:
                    d.signal = True
        import contextlib
        with contextlib.ExitStack() as es:
            esem = {e: es.enter_context(nc.semaphore("s_" + e)) for e in ("pe", "act", "dve", "pool")}
            dsem = {e: [es.enter_context(nc.semaphore("d_%s%d" % (e, i))) for i in range(n)]
                    for e, n in NDSEM.items()}
            for e in ENGS:
                cnt = 0
                k = 0
                hist = []
                for o in self.ops[e]:
                    if o.is_dma:
                        n = NDSEM[e]
                        o.sem = dsem[e][k % n]
                        o.ticket = 16 * (k // n + 1)
                        o.prev_dma = hist[k - n] if k >= n else None
                        hist.append(o)
                        k += 1
                    elif o.signal:
                        cnt += 1
                        o.ticket = cnt
                        o.sem = esem[e]
            block = es.enter_context(nc.Block())

            def run(e, eng):
                waited = {}
                lastd = {}
                for o in self.ops[e]:
                    waits = {}
                    for d in o.deps:
                        key = id(d.sem)
                        if key not in waits or waits[key][1] < d.ticket:
                            waits[key] = (d.sem, d.ticket)
                    if o.prev_dma is not None:
                        d = o.prev_dma
                        key = id(d.sem)
                        if key not in waits or waits[key][1] < d.ticket:
                            waits[key] = (d.sem, d.ticket)
                    for key, (sem, val) in waits.items():
                        if waited.get(key, 0) < val:
                            eng.wait_ge(sem, val)
                            waited[key] = val
                    inst = o.fn(eng)
                    if o.is_dma:
                        inst.then_inc(o.sem, 16)
                        lastd[id(o.sem)] = (o.sem, o.ticket)
                    elif o.signal:
                        inst.then_inc(o.sem, 1)
                for key, (sem, val) in lastd.items():
                    if waited.get(key, 0) < val:
                        eng.wait_ge(sem, val)

            @block.tensor
            def _(eng):
                run("pe", eng)

            @block.scalar
            def _(eng):
                run("act", eng)

            @block.vector
            def _(eng):
                run("dve", eng)

            @block.gpsimd
            def _(eng):
                run("pool", eng)

            @block.sync
            def _(eng):
                run("sp", eng)


class K:
    def __init__(self, stage="full"):
        self.stage = stage
        self.nc = bass.Bass("TRN2", target_bir_lowering=False)
        self.P = Prog(self.nc)
        self.dram = {}
        self._uid = 0

    def din(self, name, shape, dt=F32):
        t = self.nc.dram_tensor(name, list(shape), dt, kind="ExternalInput").ap()
        self.dram[name] = t
        return t

    def dout(self, name, shape, dt=F32):
        t = self.nc.dram_tensor(name, list(shape), dt, kind="ExternalOutput").ap()
        self.dram[name] = t
        return t

    def dscratch(self, name, shape, dt=F32):
        t = self.nc.dram_tensor(name, list(shape), dt, kind="Internal").ap()
        self.dram[name] = t
        return t

    def sb(self, name, shape, dt):
        return self.es.enter_context(self.nc.sbuf_tensor(name, list(shape), dt))

    def dma(self, q, out, in_, reads=(), writes=(), **kw):
        return self.P.op(q, lambda e: e.dma_start(out=out, in_=in_, **kw), reads, writes, dma=True)

    def mm(self, out, lhsT, rhs, start, stop, reads=(), writes=(), **kw):
        return self.P.op("pe", lambda e: e.matmul(out, lhsT, rhs, start=start, stop=stop, **kw),
                         reads, writes)

    def tr(self, out, in_, ident, reads=(), writes=()):
        return self.P.op("pe", lambda e: e.transpose(out, in_, ident), reads, writes)

    def act(self, out, in_, func, reads=(), writes=(), **kw):
        return self.P.op("act", lambda e: e.activation(out=out, in_=in_, func=func, **kw), reads, writes)

    def tt(self, eng, out, in0, in1, op, reads=(), writes=()):
        return self.P.op(eng, lambda e: e.tensor_tensor(out=out, in0=in0, in1=in1, op=op), reads, writes)

    def ts(self, eng, out, in0, s1, s2, op0, op1=None, reads=(), writes=()):
        if op1 is None:
            return self.P.op(eng, lambda e: e.tensor_scalar(out=out, in0=in0, scalar1=s1, scalar2=None, op0=op0),
                             reads, writes)
        return self.P.op(eng, lambda e: e.tensor_scalar(out=out, in0=in0, scalar1=s1, scalar2=s2, op0=op0, op1=op1),
                         reads, writes)

    def stt(self, out, in0, scalar, in1, op0, op1, reads=(), writes=()):
        return self.P.op("dve", lambda e: e.scalar_tensor_tensor(out=out, in0=in0, scalar=scalar, in1=in1,
                                                                 op0=op0, op1=op1), reads, writes)

    def copy(self, eng, out, in_, reads=(), writes=()):
        if eng == "act":
            return self.P.op("act", lambda e: e.copy(out=out, in_=in_), reads, writes)
        return self.P.op(eng, lambda e: e.tensor_copy(out=out, in_=in_), reads, writes)

    def memset(self, eng, ap, val, writes=()):
        return self.P.op(eng, lambda e: e.memset(ap, val), (), writes)


CHUNKS = [(0, 512), (512, 512), (1024, 512), (1536, 512), (2048, 32)]
WNAMES = ["ffn1_norm", "ffn1_w_gate", "ffn1_w_up", "ffn1_w_down", "mix_norm", "w_in", "w_attn_proj",
          "conv_dw_w", "conv_dw_b", "conv_ln_g", "conv_ln_b", "w_conv_proj", "w_o", "xattn_norm", "mem_norm",
          "w_xq", "w_xkv", "w_xo", "ffn2_norm", "ffn2_w_gate", "ffn2_w_up", "ffn2_w_down", "final_norm", "rel_bias"]
WSHAPES = {"ffn1_norm": [1, D], "ffn1_w_gate": [1, D, DFF], "ffn1_w_up": [1, D, DFF], "ffn1_w_down": [1, DFF, D],
           "mix_norm": [1, D], "w_in": [1, D, IN_WIDTH], "w_attn_proj": [1, 256, D], "conv_dw_w": [1, 31, 512],
           "conv_dw_b": [1, 512], "conv_ln_g": [1, 512], "conv_ln_b": [1, 512], "w_conv_proj": [1, 512, D],
           "w_o": [1, D, D], "xattn_norm": [1, D], "mem_norm": [1, D], "w_xq": [1, D, D], "w_xkv": [1, D, 2 * D],
           "w_xo": [1, D, D], "ffn2_norm": [1, D], "ffn2_w_gate": [1, D, DFF], "ffn2_w_up": [1, D, DFF],
           "ffn2_w_down": [1, DFF, D], "final_norm": [D], "rel_bias": [32, 12]}
NORM_IDX = {"ffn1_norm": 0, "mix_norm": 1, "xattn_norm": 2, "ffn2_norm": 3, "final_norm": 4, "mem_norm": 5}


class Arena:
    def __init__(self, k, nbytes):
        self.t = k.sb("arena", [128, nbytes // 2], BF16)
        self.n = nbytes
        self.off = 0

    def reset(self, off=0):
        self.off = off

    def alloc(self, shape, dt):
        esz = 4 if dt == F32 else 2
        n = esz
        for d in shape[1:]:
            n *= d
        n = (n + 63) // 64 * 64
        assert self.off + n <= self.n, ("arena overflow", self.off, n, self.n)
        v = self.t[:, self.off // 2:(self.off + n) // 2]
        self.off += n
        if dt == F32:
            v = v.bitcast(F32)
        tot = 1
        for d in shape[1:]:
            tot *= d
        v = v[:, 0:tot]
        if len(shape) == 3:
            v = v.rearrange("p (a b) -> p a b", a=shape[1])
        elif len(shape) == 4:
            v = v.rearrange("p (a b c) -> p a b c", a=shape[1], b=shape[2])
        elif len(shape) == 5:
            v = v.rearrange("p (a b c d) -> p a b c d", a=shape[1], b=shape[2], c=shape[3])
        return v[0:shape[0]]


class Builder(K):
    def build(self):
        import contextlib
        k = self
        nc = self.nc
        P = self.P
        xp = k.din("xp", [S, D])
        xs = k.din("xs", [NS_TOK, D])
        mem = k.din("mem", [256, D])
        cwk = [k.din("cw%dk" % g, [4, GROUPS[g][0], 256]) for g in range(3)]
        cwv = [k.din("cw%dv" % g, [4, GROUPS[g][0], 256]) for g in range(3)]
        stc = k.din("stc", [4, 30, 512])
        cmk = k.din("cmk", [4, 256, D])
        cmv = k.din("cmv", [4, 256, D])
        W = {n: k.din(n, WSHAPES[n]) for n in WNAMES}
        k.W = W
        ebuck = k.din("ebuck", [3, 33, 384])
        biasrep = k.dscratch("biasrep", [12, 128, 384])
        k.biasrep = biasrep
        yp = k.dout("yp", [S, D])
        ys = k.dout("ys", [NS_TOK, D])
        pwk = [k.dout("pw%dk" % g, [GROUPS[g][0], 256]) for g in range(3)]
        pwv = [k.dout("pw%dv" % g, [GROUPS[g][0], 256]) for g in range(3)]
        pconv = k.dout("pconv", [30, 512])
        pmk = k.dout("pmk", [256, D])
        pmv = k.dout("pmv", [256, D])
        swk = [k.dout("sw%dk" % g, [4, GROUPS[g][0], 256]) for g in range(3)]
        swv = [k.dout("sw%dv" % g, [4, GROUPS[g][0], 256]) for g in range(3)]
        sconv = k.dout("sconv", [4, 30, 512])
        k.io = dict(xp=xp, xs=xs, mem=mem, cwk=cwk, cwv=cwv, stc=stc, cmk=cmk, cmv=cmv, yp=yp, ys=ys,
                    pwk=pwk, pwv=pwv, pconv=pconv, pmk=pmk, pmv=pmv, swk=swk, swv=swv, sconv=sconv)
        xspill = k.dscratch("xspill", [128, 8 * NT])

        with contextlib.ExitStack() as es:
            k.es = es
            k.xT = k.sb("xT", [128, 8, NT], F32)
            k.identf = k.sb("identf", [128, 128], F32)
            k.ident = k.sb("ident", [128, 128], BF16)
            k.ones = k.sb("ones", [128, 128], BF16)
            k.mhalf = k.sb("mhalf", [128, 8], F32)
            k.onesf = k.sb("onesf", [128, 128], F32)
            k.rsc = [k.sb("rsc%d" % i, [128, 4], F32) for i in range(2)]
            k.rdg = [k.sb("rdg%d" % i, [128, 4, 128], F32) for i in range(2)]
            k.gcol = k.sb("gcol", [128, 6, 8], F32)
            k.ar = Arena(k, 137 * 1024)
            k.ps = es.enter_context(nc.psum_tensor("psall", [128, 8, 512], F32))
            k._rot = {}

            k.memset("pool", k.identf[:], 0.0, writes=["identf"])
            P.op("pool", lambda e: e.affine_select(out=k.identf[:], in_=k.identf[:], pattern=[[-1, 128]],
                                                   compare_op=ALU.not_equal, fill=1.0, base=0,
                                                   channel_multiplier=1), reads=["identf"], writes=["identf"])
            k.copy("dve", k.ident[:], k.identf[:], reads=["identf"], writes=["ident"])
            k.memset("pool", k.ones[:], 1.0, writes=["ones"])
            k.memset("pool", k.mhalf[:], -0.5, writes=["mhalf"])
            k.memset("pool", k.onesf[:], 1.0, writes=["onesf"])
            for name, gi in NORM_IDX.items():
                src = W[name]
                src = src[0, :] if len(WSHAPES[name]) == 2 else src
                k.dma("sp", k.gcol[:, gi, :], src.rearrange("(kc p) -> p kc", p=128), writes=[("gcol", gi)],
                      allow_slow_non_contiguous=True)

            BOOT0 = 33280 + 8192 + 46464 + 24576
            k.ar.reset(BOOT0)
            stg = [k.ar.alloc([128, D], F32) for _ in range(2)]
            k.boot_mark = k.ar.off
            for t in range(17):
                R = 128 if t < 16 else 32
                sgi = t % 2
                src = xp[t * 128:(t + 1) * 128, :] if t < 16 else xs[:, :]
                k.dma("sp", stg[sgi][0:R, :], src, reads=["boot"], writes=[("stg", sgi)])
                bp = 6 if t % 2 == 0 else 4
                for kc in range(8):
                    k.tr(k.ps[:, bp + kc // 4, (kc % 4) * 128:(kc % 4) * 128 + R],
                         stg[sgi][0:R, kc * 128:(kc + 1) * 128], k.identf[0:R, 0:R],
                         reads=[("stg", sgi), "identf", "boot"], writes=[("ps", bp + kc // 4)])
                src_ps = k.ps[:, bp:bp + 2, :].rearrange("p b (k r) -> p (b k) r", k=4)[:, :, 0:R]
                eng = "act" if t % 2 == 0 else "dve"
                k.copy(eng, k.xT[:, :, t * 128:t * 128 + R], src_ps, reads=[("ps", bp), ("ps", bp + 1)],
                       writes=[("xT", kc, min(t // 4, 4)) for kc in range(8)])

            if k.stage in ("mix", "full"):
                k.setup_mix(ebuck)
            if k.stage not in ("ffn", "full"):
                P.barrier()
            if k.stage in ("ffn", "full"):
                k.ffn("ffn1")
                P.barrier()
            if k.stage in ("mix", "full"):
                k.mix()
                P.barrier()
            if k.stage in ("xattn", "full"):
                k.xattn()
                P.barrier()
            if k.stage in ("ffn", "full"):
                k.ffn("ffn2")
                P.barrier()
            k.final_norm()
            P.emit()
        return nc

    def rot(self, name, n):
        i = self._rot.get(name, 0)
        self._rot[name] = i + 1
        return i % n


    def rstd_bc(self, src, srckey, ci, cs, n, sq, banks=(6, 7), fence=None):
        k = self
        i = k.rot("sq", len(sq))
        fr = [fence] if fence else []
        k.act(sq[i][:, :, 0:n], src[:, :, cs:cs + n], AF.Square,
              reads=[(srckey, kc, ci) for kc in range(8)] + fr, writes=[("sq", i)])
        nt = (n + 127) // 128
        R = min(n, 128)
        b = banks[k.rot("psn%s" % (banks,), len(banks))]
        for c in range(nt):
            for kc in range(8):
                k.mm(k.ps[0:R, b, c:c + 1], sq[i][:, kc, c * 128:c * 128 + R], k.ones[:, 0:1], kc == 0, kc == 7,
                     reads=[("sq", i), "ones"] + fr, writes=[("ps", b)])
        j = k.rot("rsc", 2)
        rsc = k.rsc[j]
        k.ts("dve", rsc[0:R, 0:nt], k.ps[0:R, b, 0:nt], 1.0 / D, EPS, ALU.mult, ALU.add,
             reads=[("ps", b)], writes=[("rsc", j)])
        k.tt("pool", rsc[0:R, 0:nt], rsc[0:R, 0:nt], k.mhalf[0:R, 0:nt], ALU.pow,
             reads=[("rsc", j), "mhalf"], writes=[("rsc", j)])
        for c in range(nt):
            k.ts("dve", k.rdg[j][0:R, c, 0:R], k.identf[0:R, 0:R], rsc[0:R, c:c + 1], None, ALU.mult,
                 reads=["identf", ("rsc", j)], writes=[("rdg", j, c)])
        for c in range(nt):
            k.mm(k.ps[:, b, c * 128:c * 128 + R], k.onesf[0:R, :], k.rdg[j][0:R, c, 0:R], True, True,
                 reads=["onesf", ("rdg", j, c)], writes=[("ps", b)])
        return k.ps[:, b, 0:n], ("ps", b)

    def norm(self, gi, hT, sq, rstd, chunks=CHUNKS, src=None, keyp="hT", srckey="xT", final=False, fence=None):
        k = self
        src = k.xT if src is None else src
        for ci, (cs, n) in enumerate(chunks):
            rb, rkey = k.rstd_bc(src, srckey, ci, cs, n, sq, fence=fence)
            for kc in range(8):
                k.stt(hT[:, kc, cs:cs + n], src[:, kc, cs:cs + n], k.gcol[:, gi, kc:kc + 1], rb,
                      ALU.mult, ALU.mult, reads=[(srckey, kc, ci), rkey, ("gcol", gi)],
                      writes=[(keyp, kc, ci)])

    def ffn(self, pre):
        k = self
        W = k.W
        wg, wu, wd = W[pre + "_w_gate"], W[pre + "_w_up"], W[pre + "_w_down"]
        ar = k.ar
        ar.reset()
        hT = ar.alloc([128, 8, NT], BF16)
        sq = [ar.alloc([128, 8, 512], BF16) for _ in range(1)]
        rstd = None
        actb = ar.alloc([128, NF, 1056], BF16)
        wgs = [ar.alloc([128, 8, 256], BF16) for _ in range(3)]
        wus = [ar.alloc([128, 8, 256], BF16) for _ in range(3)]
        wds = [ar.alloc([128, NF, 256], BF16) for _ in range(2)]
        sg = [ar.alloc([128, 512], F32) for _ in range(2)]

        def load_gu(s):
            wi = k.rot("wgu_l", 3)
            k.dma("pool", wgs[wi][:], wg[0, :, s * 256:(s + 1) * 256].rearrange("(kc p) f -> p kc f", p=128),
                  writes=[("wg", wi)])
            k.dma("pool", wus[wi][:], wu[0, :, s * 256:(s + 1) * 256].rearrange("(kc p) f -> p kc f", p=128),
                  writes=[("wu", wi)])
        nsl = NF // 2
        seq = [s_ for _ in range(2) for s_ in range(nsl)]
        nload = 0
        for _ in range(2):
            load_gu(seq[nload])
            nload += 1
        k.norm(NORM_IDX[pre + "_norm"], hT, sq, rstd)
        blocks = [(0, [0, 1]), (1024, [2, 3, 4])]
        for bi, (c0, cis) in enumerate(blocks):
            for s in range(NF // 2):
                wi = k.rot("wgu", 3)
                if nload < len(seq):
                    load_gu(seq[nload])
                    nload += 1
                for j in range(2):
                    f = 2 * s + j
                    for ci in cis:
                        cs, n = CHUNKS[ci]
                        bg = k.rot("bg", 2)
                        bu = 2 + k.rot("bu", 2)
                        for kc in range(8):
                            k.mm(k.ps[:, bg, 0:n], wgs[wi][:, kc, j * 128:(j + 1) * 128], hT[:, kc, cs:cs + n],
                                 kc == 0, kc == 7, reads=[("wg", wi), ("hT", kc, ci)], writes=[("ps", bg)])
                        for kc in range(8):
                            k.mm(k.ps[:, bu, 0:n], wus[wi][:, kc, j * 128:(j + 1) * 128], hT[:, kc, cs:cs + n],
                                 kc == 0, kc == 7, reads=[("wu", wi), ("hT", kc, ci)], writes=[("ps", bu)])
                        si = k.rot("sg", 2)
                        k.act(sg[si][:, 0:n], k.ps[:, bg, 0:n], AF.Silu, reads=[("ps", bg)], writes=[("sg", si)])
                        k.tt("dve", actb[:, f, cs - c0:cs - c0 + n], sg[si][:, 0:n], k.ps[:, bu, 0:n], ALU.mult,
                             reads=[("sg", si), ("ps", bu)], writes=[("act", f, ci)])
            for s in range(4):
                wi = k.rot("wd", 2)
                fence = ["boot"] if (pre == "ffn1" and bi == 0 and s < 2) else []
                k.dma("pool", wds[wi][:], wd[0, :, s * 256:(s + 1) * 256].rearrange("(f p) c -> p f c", p=128),
                      writes=[("wd", wi)] + fence)
                for j in range(2):
                    oc = 2 * s + j
                    for ci in cis:
                        cs, n = CHUNKS[ci]
                        bd = 4 + k.rot("bd", 2)
                        for f in range(NF):
                            k.mm(k.ps[:, bd, 0:n], wds[wi][:, f, j * 128:(j + 1) * 128],
                                 actb[:, f, cs - c0:cs - c0 + n], f == 0, f == NF - 1,
                                 reads=[("wd", wi), ("act", f, ci)], writes=[("ps", bd)])
                        k.stt(k.xT[:, oc, cs:cs + n], k.ps[:, bd, 0:n], 0.5, k.xT[:, oc, cs:cs + n],
                              ALU.mult, ALU.add, reads=[("ps", bd), ("xT", oc, ci)], writes=[("xT", oc, ci)])


    def setup_mix(self, ebuck):
        k = self
        P = k.P
        ar = k.ar
        io = k.io
        ar.reset(k.boot_mark)
        rb = ar.alloc([33, 12], F32)
        rbb = ar.alloc([33, 12, 128], F32)
        eb = ar.alloc([33, 3, 384], F32)
        wrep = [ar.alloc([128, 384], F32) for _ in range(2)]
        P.op("pool", lambda e: e.memset(rb[:], 1.0), ["boot"], ["rb"])
        k.dma("sp", rb[0:32, :], k.W["rel_bias"][:, :], writes=["rb"], reads=["rb", "boot"])
        k.dma("sp", eb[:], ebuck.rearrange("g b u -> b g u"), reads=["boot"], writes=["eb"])
        k.copy("dve", rbb[:], rb[:].unsqueeze(2).to_broadcast([33, 12, 128]), reads=["rb", "boot"], writes=["rbb"])
        for c in range(12):
            g = c // 4
            b = c % 2
            k.mm(k.ps[:, b, 0:384], rbb[:, c, :], eb[:, g, :], True, True, reads=["rbb", "eb", "boot"], writes=[("ps", b)])
            k.copy("act", wrep[b][:], k.ps[:, b, 0:384], reads=[("ps", b), "boot"], writes=[("wrep", b)])
            k.dma("sp", k.biasrep[c], wrep[b][:], reads=[("wrep", b), "boot"], writes=[("biasrep", c)])

    def shift_copies(self):
        k = self
        io = k.io
        jobs = []
        for g in (2, 1, 0):
            Lb = GROUPS[g][0]
            for bb in range(4):
                for src, dst in ((io["cwk"][g], io["swk"][g]), (io["cwv"][g], io["swv"][g])):
                    jobs.append((dst[bb, 0:Lb - 8, :].rearrange("(a b) c -> a (b c)", b=8),
                                 src[bb, 8:Lb, :].rearrange("(a b) c -> a (b c)", b=8)))
        for bb in range(4):
            jobs.append((io["sconv"][bb, 0:22, :], io["stc"][bb, 8:30, :]))
        return jobs

    def mix(self):
        k = self
        P = k.P
        W = k.W
        io = k.io
        ar = k.ar
        win = W["w_in"][0]

        def sl(s0, n, st):
            return slice(s0, s0 + (n - 1) * st + 1, st)

        import os
        msub = 9
        if msub < 1:
            return
        ar.reset()
        hT = ar.alloc([128, 8, NT], BF16)
        oT = ar.alloc([128, 2, NT], BF16)
        mark0 = ar.off
        sq = [ar.alloc([128, 8, 512], BF16)]
        rstd = [ar.alloc([128, 512], F32) for _ in range(2)]
        k.norm(NORM_IDX["mix_norm"], hT, sq, rstd, fence="mixboot")
        ar.reset(mark0)
        NZ = ar.alloc([128, 2, 2, NT], F32)
        mark1 = ar.off
        k.wslab = [ar.alloc([128, 8, 256], BF16) for _ in range(2)]
        QT = ar.alloc([128, 2, NT], BF16)
        KT = ar.alloc([128, 2, S], BF16)
        Vp = ar.alloc([128, 16, 4, 128], BF16)
        bias = ar.alloc([128, 2, 2, 2, 128], F32)
        PTb = [ar.alloc([128, 512], BF16) for _ in range(4)]
        kvw = oT.rearrange("p a t -> p (a t)")[:, 0:4096].rearrange("p (k c) -> p k c", k=8)
        kvst = [ar.alloc([128, 512], F32) for _ in range(2)]
        ktok = [ar.alloc([128, 256], BF16) for _ in range(3)]
        opad = ar.alloc([128, 2, 128], BF16)
        ks_bf = ar.alloc([32, 256], BF16)
        vs_bf = ar.alloc([32, 256], BF16)
        KsT = ar.alloc([128, 2, 32], BF16)
        KcT8 = ar.alloc([128, 8, 2, 128], BF16)
        Vpn = [ar.alloc([8, 4, 128], BF16) for _ in range(2)]
        sbias = ar.alloc([128, 2, 9, 2, 8], F32)
        stmp = ar.alloc([8, 2, 2, 2, 8], F32)
        vp_flat = Vp.rearrange("p a b c -> p (a b c)")
        kt_flat = KT.rearrange("p a t -> p (a t)")
        Vpc8 = [vp_flat[:, 0:4096].rearrange("p (c h x) -> p c h x", c=8, h=4),
                kt_flat[:, 0:4096].rearrange("p (c h x) -> p c h x", c=8, h=4)]
        kc8 = [vp_flat[:, 4096:6144].rearrange("p (c x) -> p c x", c=8),
               QT[:, 0, 0:2048].rearrange("p (c x) -> p c x", c=8)]
        vc8 = [vp_flat[:, 6144:8192].rearrange("p (c x) -> p c x", c=8),
               QT[:, 1, 0:2048].rearrange("p (c x) -> p c x", c=8)]

        k.memset("pool", Vp[:], 0.0, writes=[("Vp", b) for b in range(16)])
        for i in range(2):
            k.memset("pool", Vpn[i][:], 0.0, writes=[("Vpn", i)])
        k.memset("pool", opad[:], 0.0, writes=["opad"])
        k.memset("pool", opad[:, 0, 0:64], 1.0, writes=["opad"])
        k.memset("pool", opad[:, 1, 64:128], 1.0, writes=["opad"])

        def pad_copy(eng, dst, src256, rows, rkeys, wkeys):
            d2 = dst[0:rows].rearrange("p (a b) c -> p a b c", b=2)
            s2 = src256[0:rows].rearrange("p (a b d) -> p a b d", a=2, b=2)
            for par in range(2):
                k.copy(eng, d2[:, :, par, par * 64:(par + 1) * 64], s2[:, :, par, :], reads=rkeys, writes=wkeys)

        def attn_block(g_first, qap, nq, keyblocks, evac_cols, qkeys, tag, fullbias=None):
            st_ = k.rot("aset", 2)
            sbanks = (0, 1) if st_ == 0 else (4, 5)
            nb = len(keyblocks)
            for par in range(2):
                bk = sbanks[par]
                for bi, (nk, ktap, vp, bvf, rkeys) in enumerate(keyblocks):
                    for a in range(2):
                        h = 2 * a + par
                        sl0 = (bi * 2 + a) * nq
                        k.mm(k.ps[0:nk, bk, sl0:sl0 + nq], ktap(h), qap(h), True, True,
                             reads=list(rkeys) + list(qkeys), writes=[("ps", bk)])
                same = all(kb[0] == keyblocks[0][0] for kb in keyblocks)
                if fullbias is not None:
                    ncol = nb * 2 * nq
                    pv = k.ps[:, bk, 0:ncol]
                    k.tt("dve", pv, pv, fullbias(par), ALU.add, reads=[("ps", bk), "sbias"], writes=[("ps", bk)])
                    k.act(PTb[st_ * 2 + par][:, 0:ncol], pv, AF.Exp, reads=[("ps", bk)], writes=[("PTb", st_, par)])
                elif same and nb == 2 and nq == 128:
                    nk = keyblocks[0][0]
                    pv = k.ps[0:nk, bk, 0:4 * nq]
                    k.tt("dve", pv, pv, bias[:, par, :, :, :].rearrange("p w a q -> p (w a q)"), ALU.add,
                         reads=[("ps", bk), "bias"], writes=[("ps", bk)])
                    k.act(PTb[st_ * 2 + par][0:nk, 0:4 * nq], pv, AF.Exp, reads=[("ps", bk)],
                          writes=[("PTb", st_, par)])
                else:
                    for bi, (nk, ktap, vp, bvf, rkeys) in enumerate(keyblocks):
                        pv = k.ps[0:nk, bk, bi * 2 * nq:(bi * 2 + 2) * nq].rearrange("p (a q) -> p a q", a=2)
                        k.tt("dve", pv, pv, bvf(par), ALU.add, reads=[("ps", bk), "bias"], writes=[("ps", bk)])
                        k.act(PTb[st_ * 2 + par][0:nk, bi * 2 * nq:(bi * 2 + 2) * nq],
                              k.ps[0:nk, bk, bi * 2 * nq:(bi * 2 + 2) * nq], AF.Exp, reads=[("ps", bk)],
                              writes=[("PTb", st_, par)])
            return (st_, g_first, nq, keyblocks, evac_cols)

        def attn_B(ctx):
            st_, g_first, nq, keyblocks, evac_cols = ctx
            pvb = 6 + st_
            nb = len(keyblocks)
            for nz in range(2):
                for pair in range(2):
                    first = True
                    col = (nz * 2 + pair) * nq
                    for bi, (nk, ktap, vp, bvf, rkeys) in enumerate(keyblocks):
                        for hh in range(2):
                            h = pair * 2 + hh
                            lhs = vp[0:nk, h, :] if nz == 0 else opad[0:nk, hh, :]
                            last = (bi == nb - 1) and hh == 1
                            pcol = (bi * 2 + pair) * nq
                            k.mm(k.ps[:, pvb, col:col + nq], lhs, PTb[st_ * 2 + hh][0:nk, pcol:pcol + nq], first, last,
                                 reads=list(rkeys) + [("PTb", st_, hh), "opad"], writes=[("ps", pvb)])
                            first = False
            src = k.ps[:, pvb, 0:4 * nq].rearrange("p (a b q) -> p a b q", a=2, b=2)
            dst = evac_cols
            k._pace = k._pace + 1 if hasattr(k, "_pace") else 0
            if g_first:
                k.copy("dve", dst, src, reads=[("ps", pvb)], writes=["NZ", ("pace", k._pace), "mixboot"])
            else:
                k.tt("dve", dst, src, dst, ALU.add, reads=[("ps", pvb), "NZ"], writes=["NZ", ("pace", k._pace)])

        for g, (win_len, d) in enumerate(GROUPS):
            L = S // d
            for which in range(2):
                for par in range(2):
                    src = bass.AP(tensor=k.biasrep.tensor,
                                  offset=(4 * g + par) * 128 * 384 + (127 if which == 0 else 255),
                                  ap=[[383, 128], [2 * 128 * 384, 2], [1, 128]])
                    k.dma("sp", bias[:, par, which, :, :], src, reads=["biasrep"], writes=["bias"])
            if g == 0:
                shift_jobs = k.shift_copies()
            C = min(d, 8)
            nqc = 8 // C
            NB = C + 1
            k.memset("pool", sbias[:], NEG, writes=["sbias"])
            for r in range(C):
                k.copy("dve", sbias[:, :, r, :, r:8:d] if d < 8 else sbias[:, :, r, :, r:r + 1],
                       bias[:, :, 1, :, 0:nqc], reads=["bias", "sbias"], writes=["sbias"])
            cur8 = bias[0:8, :, 0, :, 0:8]
            if d == 1:
                k.copy("dve", sbias[0:8, :, C, :, :], cur8, reads=["bias", "sbias"], writes=["sbias"])
            else:
                k.copy("dve", stmp[:, 0], cur8, reads=["bias"], writes=["stmp0"])
                st0 = stmp[:, 0].rearrange("p a b q -> p (a b) q")
                P.op("pool", lambda e: e.affine_select(out=st0, in_=st0, pattern=[[0, 4], [-1, 8]],
                                                       compare_op=ALU.is_equal, fill=NEG, base=0, channel_multiplier=1),
                     reads=["stmp0"], writes=["stmp0"])
                if d == 4:
                    k.memset("pool", stmp[:, 1], NEG, writes=["stmp1"])
                    k.copy("dve", stmp[:, 1, :, :, 3:8], bias[0:8, :, 0, :, 0:5], reads=["bias", "stmp1"], writes=["stmp1"])
                    st1 = stmp[:, 1].rearrange("p a b q -> p (a b) q")
                    P.op("pool", lambda e: e.affine_select(out=st1, in_=st1, pattern=[[0, 4], [-1, 8]],
                                                           compare_op=ALU.is_equal, fill=NEG, base=4, channel_multiplier=1),
                         reads=["stmp1"], writes=["stmp1"])
                    k.tt("dve", sbias[0:8, :, C, :, :], stmp[:, 0], stmp[:, 1], ALU.max, reads=["stmp0", "stmp1", "sbias"],
                         writes=["sbias"])
                else:
                    k.copy("dve", sbias[0:8, :, C, :, :], stmp[:, 0], reads=["stmp0", "sbias"], writes=["sbias"])

            def evac_q(oc, ci, cs, n, b, d=d, L=L):
                if ci < 4:
                    dst = QT[:, oc, 0:S].rearrange("p (r j) -> p r j", r=d)[:, :, cs // d:(cs + n) // d]
                    srcp = k.ps[:, b, 0:n].rearrange("p (jj r) -> p r jj", r=d)
                else:
                    dst = QT[:, oc, cs:cs + n]
                    srcp = k.ps[:, b, 0:n]
                k.act(dst, srcp, AF.Copy, reads=[("ps", b)], writes=[("QT", oc, ci)], scale=0.125)
            wi_q = k.rot("wq", 2)
            wtq = k.wslab[wi_q]
            k.dma("pool", wtq[:, :, 0:256], win[:, g * 256:(g + 1) * 256].rearrange("(kc p) f -> p kc f", p=128),
                  writes=[("wslab", wi_q)])
            k.dma("pool", kvw[:, :, 0:256], win[:, 768 + g * 256:768 + (g + 1) * 256].rearrange("(kc p) f -> p kc f", p=128),
                  writes=["kvw"])
            k.dma("pool", kvw[:, :, 256:512], win[:, 1536 + g * 256:1536 + (g + 1) * 256].rearrange("(kc p) f -> p kc f", p=128),
                  writes=["kvw"], reads=["kvw"])
            keep0 = S - win_len
            qunits = [(oc, ci) for oc in range(2) for ci in range(5)]

            def q_unit(oc, ci):
                cs, n = CHUNKS[ci]
                b = k.rot("qb", 2)
                for kc in range(8):
                    k.mm(k.ps[:, b, 0:n], wtq[:, kc, oc * 128:(oc + 1) * 128], hT[:, kc, cs:cs + n], kc == 0, kc == 7,
                         reads=[("wslab", wi_q), ("hT", kc, ci)], writes=[("ps", b)])
                evac_q(oc, ci, cs, n, b)

            def kt_transposes(blk, ti, r, b_):
                psb = k.ps[:, 7, :].bitcast(BF16)
                for pair in range(2):
                    k.tr(psb[:, pair * 128:(pair + 1) * 128], ktok[ti][:, pair * 128:(pair + 1) * 128], k.ident[:],
                         reads=[("ktok", ti), "ident"], writes=[("ps", 7)])
                kpos = r * L + 128 * b_
                k.copy("dve", KT[:, :, kpos:kpos + 128], psb[:, 0:256].rearrange("p (a m) -> p a m", a=2),
                       reads=[("ps", 7)], writes=[("KT", blk)])
            lag = None
            for blk in range(17):
                if blk < 16:
                    r, b_ = blk // (16 // d), blk % (16 // d)
                    rows = 128
                    tsel = sl(r + 128 * b_ * d, 128, d)
                else:
                    rows = 32
                    tsel = slice(S, S + 32)
                pb = 2 + k.rot("pbkv2", 2)
                for kc in range(8):
                    k.mm(k.ps[0:rows, pb, :], hT[:, kc, tsel], kvw[:, kc, :], kc == 0, kc == 7,
                         reads=["kvw"] + [("hT", kc, ci) for ci in range(5)], writes=[("ps", pb)])
                si = k.rot("kvst", 2)
                k.copy("act", kvst[si][0:rows, :], k.ps[0:rows, pb, :], reads=[("ps", pb)], writes=[("kvst", si)])
                if qunits:
                    q_unit(*qunits.pop(0))
                if lag is not None:
                    kt_transposes(*lag)
                    lag = None
                if blk < 16:
                    t0 = r + 128 * b_ * d
                    if t0 >= keep0:
                        k.dma("sp", io["pwk"][g][sl(t0 - keep0, 128, d), :], kvst[si][:, 0:256], reads=[("kvst", si)])
                        k.dma("sp", io["pwv"][g][sl(t0 - keep0, 128, d), :], kvst[si][:, 256:512], reads=[("kvst", si)])
                    ti = k.rot("ktok", 3)
                    k.copy("dve", ktok[ti][:], kvst[si][:, 0:256], reads=[("kvst", si)], writes=[("ktok", ti)])
                    pad_copy("pool", Vp[:, blk], kvst[si][:, 256:512], 128, [("kvst", si)], [("Vp", blk)])
                    lag = (blk, ti, r, b_)
                else:
                    for bb in range(4):
                        Lb = win_len
                        k.dma("sp", io["swk"][g][bb, Lb - 8:Lb, :], kvst[si][bb * 8:(bb + 1) * 8, 0:256],
                              reads=[("kvst", si)])
                        k.dma("sp", io["swv"][g][bb, Lb - 8:Lb, :], kvst[si][bb * 8:(bb + 1) * 8, 256:512],
                              reads=[("kvst", si)])
                    k.copy("pool", ks_bf[:], kvst[si][0:32, 0:256], reads=[("kvst", si)], writes=["ks_bf"])
                    k.copy("pool", vs_bf[:], kvst[si][0:32, 256:512], reads=[("kvst", si)], writes=["vs_bf"])
                    psb = k.ps[:, 7, :].bitcast(BF16)
                    for pair in range(2):
                        k.tr(psb[:, pair * 128:pair * 128 + 32], ks_bf[0:32, pair * 128:(pair + 1) * 128],
                             k.ident[0:32, 0:32], reads=["ks_bf", "ident"], writes=[("ps", 7)])
                    k.copy("dve", KsT[:], psb[:, 0:256].rearrange("p (a m) -> p a m", a=2)[:, :, 0:32],
                           reads=[("ps", 7)], writes=["KsT"])
            while qunits:
                q_unit(*qunits.pop(0))
            if msub < 2:
                return
            pend = None
            for blk in range(16):
                r, b_ = blk // (16 // d), blk % (16 // d)
                qpos = r * L + 128 * b_
                t0 = r + 128 * b_ * d

                def qap(h, qpos=qpos):
                    return QT[(h % 2) * 64:(h % 2) * 64 + 64, h // 2, qpos:qpos + 128]

                def mk_kt(pos):
                    return lambda h: KT[(h % 2) * 64:(h % 2) * 64 + 64, h // 2, pos:pos + 128]
                kbs = [(128, mk_kt(qpos), Vp[:, blk], (lambda par: bias[:, par, 0, :, :]), [("KT", blk), ("Vp", blk)])]
                if b_ > 0:
                    kbs.append((128, mk_kt(qpos - 128), Vp[:, blk - 1], (lambda par: bias[:, par, 1, :, :]),
                                [("KT", blk - 1), ("Vp", blk - 1)]))
                dst = NZ[:, :, :, sl(t0, 128, d)]
                qk = [("QT", oc, ci) for oc in range(2) for ci in range(4)]
                ctx_new = attn_block(g == 0, qap, 128, kbs, dst, qk, None)
                if pend is not None:
                    attn_B(pend)
                    if shift_jobs:
                        dd, ss_ = shift_jobs.pop(0)
                        k.dma("sp", dd, ss_, reads=[("pace", k._pace)])
                pend = ctx_new
            if pend is not None:
                attn_B(pend)
                pend = None
            if msub < 3:
                continue
            P.barrier()
            for bb in range(4):
                i_ = k.rot("cset", 2)
                rowsel = [[d * 256, 128], [256, C], [1, 256]]
                srck = bass.AP(tensor=io["cwk"][g].tensor, offset=io["cwk"][g][bb].offset, ap=rowsel)
                srcv = bass.AP(tensor=io["cwv"][g].tensor, offset=io["cwv"][g][bb].offset, ap=rowsel)
                k.dma("pool", kc8[i_][:, 0:C, :], srck, writes=[("kc8", i_)])
                k.dma("pool", vc8[i_][:, 0:C, :], srcv, writes=[("vc8", i_)])
                d2 = Vpc8[i_][:, 0:C].rearrange("p c (a b) x -> p c a b x", b=2)
                s2 = vc8[i_][:, 0:C, :].rearrange("p c (a b x) -> p c a b x", a=2, b=2)
                if bb < 2:
                    k.memset("pool", Vpc8[i_][:, 0:C], 0.0, writes=[("Vpc8", i_)])
                for par in range(2):
                    k.copy("pool", d2[:, :, :, par, par * 64:(par + 1) * 64], s2[:, :, :, par, :],
                           reads=[("vc8", i_)], writes=[("Vpc8", i_)])
                psb2 = k.ps[:, 2:4, :].rearrange("p b x -> p (b x)").bitcast(BF16)
                for r in range(C):
                    for pair in range(2):
                        o_ = (r * 2 + pair) * 128
                        k.tr(psb2[:, o_:o_ + 128], kc8[i_][:, r, pair * 128:(pair + 1) * 128], k.ident[:],
                             reads=[("kc8", i_), "ident"], writes=[("ps", 2 + o_ // 1024)])
                k.copy("dve", KcT8[:, 0:C], psb2[:, 0:C * 256].rearrange("p (c a m) -> p c a m", c=C, a=2),
                       reads=[("ps", 2), ("ps", 3)], writes=["KcT8"])
                k.mm(k.ps[0:8, 3, 256:512], k.ident[0:32, bb * 8:(bb + 1) * 8], vs_bf[0:32, :], True, True,
                     reads=["ident", "vs_bf"], writes=[("ps", 3)])
                pad_copy("act", Vpn[i_], k.ps[0:8, 3, 256:512], 8, [("ps", 3)], [("Vpn", i_)])

                def qap(h, bb=bb):
                    return QT[(h % 2) * 64:(h % 2) * 64 + 64, h // 2, S + bb * 8:S + bb * 8 + 8]
                kbs = []
                for r in range(C):
                    kbs.append((128, (lambda h, r=r: KcT8[(h % 2) * 64:(h % 2) * 64 + 64, r, h // 2, :]), Vpc8[i_][:, r],
                                None, ["KcT8", ("Vpc8", i_)]))
                kbs.append((8, (lambda h, bb=bb: KsT[(h % 2) * 64:(h % 2) * 64 + 64, h // 2, bb * 8:(bb + 1) * 8]),
                            Vpn[i_], None, ["KsT", ("Vpn", i_)]))
                dst = NZ[:, :, :, S + bb * 8:S + bb * 8 + 8]
                ctx_new = attn_block(g == 0, qap, 8, kbs, dst, [("QT", oc, 4) for oc in range(2)], None,
                                     fullbias=(lambda par, NB=NB: sbias[:, par, 0:NB, :, :].rearrange("p b a q -> p (b a q)")))
                if pend is not None:
                    attn_B(pend)
                pend = ctx_new
            if pend is not None:
                attn_B(pend)
                pend = None
            P.barrier()
            if g < 2:
                k.memset("pool", Vp[:], 0.0, writes=[("Vp", b) for b in range(16)])

        while shift_jobs:
            dd, ss_ = shift_jobs.pop(0)
            k.dma("sp", dd, ss_, reads=[("pace", k._pace)])
        P.barrier()
        ar.reset(mark1)
        P.op("dve", lambda e: e.reciprocal(out=NZ[:, 1, :, :], in_=NZ[:, 1, :, :]), reads=["NZ"], writes=["NZ"])
        k.tt("dve", oT[:], NZ[:, 0, :, :], NZ[:, 1, :, :], ALU.mult, reads=["NZ"],
             writes=[("oT", a, ci) for a in range(2) for ci in range(5)])
        k.mix_tail(hT, oT, mark0, msub)

    def mix_tail(self, hT, oT, mark0, msub=9):
        k = self
        P = k.P
        W = k.W
        io = k.io
        ar = k.ar
        win = W["w_in"][0]
        cT = ar.alloc([128, 4, NT], BF16)
        mark = ar.off
        uT = ar.alloc([128, 4, 32 + S], BF16)
        usT = ar.alloc([128, 4, 4, 38], BF16)
        ulast = ar.alloc([128, 4, 32], F32)
        unew = ar.alloc([128, 4, 32], F32)
        markb = ar.off
        k.wslab = [ar.alloc([128, 8, 256], BF16) for _ in range(2)]
        wsl2 = [ar.alloc([128, 8, 256], BF16) for _ in range(2)]
        sgb = [ar.alloc([128, 512], F32) for _ in range(2)]
        stg = ar.alloc([128, 512], F32)
        ostg = ar.alloc([32, 512], F32)

        k.memset("pool", uT[:, :, 0:32], 0.0, writes=["uTpad"])
        k.dma("sp", stg[0:120, :], io["stc"].rearrange("b r c -> (b r) c"), writes=["stg"])
        for oc in range(4):
            k.tr(k.ps[:, 7, oc * 128:oc * 128 + 120], stg[0:120, oc * 128:(oc + 1) * 128], k.identf[0:120, 0:120],
                 reads=["stg", "identf"], writes=[("ps", 7)])
        k.copy("act", usT[:, :, :, 0:30], k.ps[:, 7, :].rearrange("p (oc x) -> p oc x", oc=4)[:, :, 0:120]
               .rearrange("p oc (b r) -> p oc b r", b=4), reads=[("ps", 7)], writes=["usT"])

        for sl_ in range(2):
            wi = k.rot("wq", 2)
            k.dma("pool", k.wslab[wi][:], win[:, 2304 + sl_ * 256:2304 + (sl_ + 1) * 256].rearrange("(kc p) f -> p kc f", p=128),
                  writes=[("wslab", wi)])
            k.dma("pool", wsl2[wi][:], win[:, 2816 + sl_ * 256:2816 + (sl_ + 1) * 256].rearrange("(kc p) f -> p kc f", p=128),
                  writes=[("wsl2", wi)])
            for j in range(2):
                oc = sl_ * 2 + j
                for ci in range(5):
                    cs, n = CHUNKS[ci]
                    ba, bb_ = k.rot("bua", 2), 2 + k.rot("bub", 2)
                    for kc in range(8):
                        k.mm(k.ps[:, ba, 0:n], k.wslab[wi][:, kc, j * 128:(j + 1) * 128], hT[:, kc, cs:cs + n], kc == 0, kc == 7,
                             reads=[("wslab", wi), ("hT", kc, ci)], writes=[("ps", ba)])
                    for kc in range(8):
                        k.mm(k.ps[:, bb_, 0:n], wsl2[wi][:, kc, j * 128:(j + 1) * 128], hT[:, kc, cs:cs + n], kc == 0, kc == 7,
                             reads=[("wsl2", wi), ("hT", kc, ci)], writes=[("ps", bb_)])
                    si = k.rot("sgb", 2)
                    k.act(sgb[si][:, 0:n], k.ps[:, bb_, 0:n], AF.Sigmoid, reads=[("ps", bb_)], writes=[("sgb", si)])
                    if ci < 4:
                        k.tt("dve", uT[:, oc, 32 + cs:32 + cs + n], k.ps[:, ba, 0:n], sgb[si][:, 0:n], ALU.mult,
                             reads=[("ps", ba), ("sgb", si)], writes=[("uT", oc, ci)])
                        if ci == 3:
                            k.tt("dve", ulast[:, oc, :], k.ps[:, ba, n - 32:n], sgb[si][:, n - 32:n], ALU.mult,
                                 reads=[("ps", ba), ("sgb", si)], writes=["ulast"])
                    else:
                        k.tt("dve", unew[:, oc, :], k.ps[:, ba, 0:32], sgb[si][:, 0:32], ALU.mult,
                             reads=[("ps", ba), ("sgb", si)], writes=["unew"])
                        k.copy("dve", usT[:, oc, :, 30:38], unew[:, oc, :].rearrange("p (b t) -> p b t", b=4),
                               reads=["unew", "usT"], writes=["usT"])
        for which, srcu in ((0, ulast), (1, unew)):
            for oc in range(4):
                k.tr(k.ps[0:32, 7, oc * 128:(oc + 1) * 128], srcu[:, oc, :], k.identf[:],
                     reads=["ulast", "unew", "identf"], writes=[("ps", 7)])
            k.copy("act", ostg[:], k.ps[0:32, 7, :], reads=[("ps", 7)], writes=["ostg"])
            if which == 0:
                k.dma("sp", io["pconv"][:, :], ostg[2:32, :], reads=["ostg"])
            else:
                for bq in range(4):
                    k.dma("sp", io["sconv"][bq, 22:30, :], ostg[bq * 8:(bq + 1) * 8, :], reads=["ostg"])

        if msub < 6:
            return
        P.barrier()
        ar.reset(mark0)
        dg = ar.alloc([128, 4, 31, 128], BF16)
        ar.reset(markb)
        wT = ar.alloc([128, 4, 31], F32)
        cvec = ar.alloc([128, 3, 4], F32)
        onesf = k.onesf
        ybuf = ar.alloc([128, 4, 512], F32)
        ysqf = ar.alloc([128, 4, 512], F32)
        st = [ar.alloc([128, 16], F32)]
        tb = [ar.alloc([128, 512], F32) for _ in range(2)]
        k.dma("sp", ybuf[0:31, 0, :], W["conv_dw_w"][0], writes=[("ybuf", 0)])
        for oc in range(4):
            k.tr(k.ps[:, 7, oc * 32:oc * 32 + 31], ybuf[0:31, 0, oc * 128:(oc + 1) * 128], k.identf[0:31, 0:31],
                 reads=[("ybuf", 0), "identf"], writes=[("ps", 7)])
        k.copy("act", wT[:], k.ps[:, 7, 0:128].rearrange("p (oc x) -> p oc x", oc=4)[:, :, 0:31], reads=[("ps", 7)],
               writes=["wT"])
        for i, nm in enumerate(("conv_dw_b", "conv_ln_g", "conv_ln_b")):
            k.dma("sp", cvec[:, i, :], W[nm][0].rearrange("(oc p) -> p oc", p=128), writes=[("cvec", i)],
                  allow_slow_non_contiguous=True)
        for oc in range(4):
            k.tt("dve" if oc % 2 == 0 else "pool", dg[:, oc, :, :],
                 k.identf[:].unsqueeze(1).to_broadcast([128, 31, 128]),
                 wT[:, oc, :].unsqueeze(2).to_broadcast([128, 31, 128]), ALU.mult,
                 reads=["identf", "wT"], writes=[("dg", oc)])
        def conv_chunk(ci):
            cs, n = CHUNKS[ci]
            for oc in range(4):
                b = 0 + k.rot("bcv", 2)
                for kk_ in range(31):
                    if ci < 4:
                        rhs = uT[:, oc, cs + kk_ + 2:cs + kk_ + 2 + n]
                        rk = [("uT", oc, c2) for c2 in range(4)] + ["uTpad"]
                    else:
                        rhs = usT[:, oc, :, kk_:kk_ + 8]
                        rk = ["usT"]
                    outp = k.ps[:, b, 0:n] if ci < 4 else k.ps[:, b, 0:32].rearrange("p (b t) -> p b t", b=4)
                    k.mm(outp, dg[:, oc, kk_, :], rhs, kk_ == 0, kk_ == 30, reads=rk + [("dg", oc)], writes=[("ps", b)])
                k.ts("dve", ybuf[:, oc, 0:n], k.ps[:, b, 0:n], cvec[:, 0, oc:oc + 1], None, ALU.add,
                     reads=[("ps", b), ("cvec", 0)], writes=[("ybuf", oc)])
                k.act(ysqf[:, oc, 0:n], ybuf[:, oc, 0:n], AF.Square, reads=[("ybuf", oc)], writes=[("ysqf", oc)])
            ntl = (n + 127) // 128
            R = min(n, 128)
            for which in range(2):
                for c in range(ntl):
                    for o2 in range(4):
                        lhs = ybuf[:, o2, c * 128:c * 128 + R] if which == 0 else ysqf[:, o2, c * 128:c * 128 + R]
                        k.mm(k.ps[0:R, 2, which * 4 + c:which * 4 + c + 1], lhs, k.onesf[:, 0:1], o2 == 0, o2 == 3,
                             reads=["onesf", ("ybuf", o2), ("ysqf", o2)], writes=[("ps", 2)])
            cs_ = st[0]
            k.ts("dve", cs_[0:R, 0:8], k.ps[0:R, 2, 0:8], 1.0 / 512, None, ALU.mult, reads=[("ps", 2)], writes=["cst"])
            k.tt("dve", cs_[0:R, 8:8 + ntl], cs_[0:R, 0:ntl], cs_[0:R, 0:ntl], ALU.mult, reads=["cst"], writes=["cst"])
            k.tt("dve", cs_[0:R, 4:4 + ntl], cs_[0:R, 4:4 + ntl], cs_[0:R, 8:8 + ntl], ALU.subtract, reads=["cst"], writes=["cst"])
            k.ts("dve", cs_[0:R, 4:4 + ntl], cs_[0:R, 4:4 + ntl], EPS, None, ALU.add, reads=["cst"], writes=["cst"])
            k.tt("pool", cs_[0:R, 4:4 + ntl], cs_[0:R, 4:4 + ntl], k.mhalf[0:R, 0:ntl], ALU.pow, reads=["cst", "mhalf"], writes=["cst"])
            k.stt(cs_[0:R, 8:8 + ntl], cs_[0:R, 0:ntl], -1.0, cs_[0:R, 4:4 + ntl], ALU.mult, ALU.mult, reads=["cst"], writes=["cst"])
            for which, off in ((0, 4), (1, 8)):
                for c in range(ntl):
                    k.ts("dve", k.rdg[which][0:R, c, 0:R], k.identf[0:R, 0:R], cs_[0:R, off + c:off + c + 1], None, ALU.mult,
                         reads=["identf", "cst"], writes=[("rdg", which, c)])
                for c in range(ntl):
                    k.mm(k.ps[:, 2 + which, c * 128:c * 128 + R], k.onesf[0:R, :], k.rdg[which][0:R, c, 0:R], True, True,
                         reads=["onesf", ("rdg", which, c)], writes=[("ps", 2 + which)])
            rs = k.ps[:, 2, :]
            nb = k.ps[:, 3, :]
            for oc in range(4):
                ti = k.rot("tb", 2)
                k.tt("dve", tb[ti][:, 0:n], ybuf[:, oc, 0:n], rs[:, 0:n], ALU.mult, reads=[("ybuf", oc), ("ps", 2)],
                     writes=[("tb", ti)])
                k.tt("dve", tb[ti][:, 0:n], tb[ti][:, 0:n], nb[:, 0:n], ALU.add, reads=[("tb", ti), ("ps", 3)],
                     writes=[("tb", ti)])
                k.act(cT[:, oc, cs:cs + n], tb[ti][:, 0:n], AF.Silu, reads=[("tb", ti), ("cvec", 1), ("cvec", 2)],
                      writes=[("cT", oc, ci)], scale=cvec[:, 1, oc:oc + 1], bias=cvec[:, 2, oc:oc + 1])
        for ci in range(5):
            conv_chunk(ci)
        P.barrier()

        if msub < 7:
            return
        ar.reset(mark0)
        merged = ar.alloc([128, 8, NT], BF16)
        ar.reset(mark)
        k.wslab = [ar.alloc([128, 8, 256], BF16) for _ in range(2)]
        wsl2 = [ar.alloc([128, 8, 256], BF16) for _ in range(2)]
        wap = ar.alloc([128, 2, D], BF16)
        wcp = ar.alloc([128, 4, D], BF16)
        sga = [ar.alloc([128, 512], F32) for _ in range(2)]
        sgb2 = [ar.alloc([128, 512], F32) for _ in range(2)]
        m1 = [ar.alloc([128, 512], F32) for _ in range(2)]
        m2 = [ar.alloc([128, 512], F32) for _ in range(2)]
        k.dma("pool", wap[:], W["w_attn_proj"][0].rearrange("(kc p) f -> p kc f", p=128), writes=["wap"])
        k.dma("pool", wcp[:], W["w_conv_proj"][0].rearrange("(kc p) f -> p kc f", p=128), writes=["wcp"])
        for sl_ in range(4):
            wi = k.rot("wq", 2)
            k.dma("pool", k.wslab[wi][:], win[:, 3328 + sl_ * 256:3328 + (sl_ + 1) * 256].rearrange("(kc p) f -> p kc f", p=128),
                  writes=[("wslab", wi)])
            k.dma("pool", wsl2[wi][:], win[:, 4352 + sl_ * 256:4352 + (sl_ + 1) * 256].rearrange("(kc p) f -> p kc f", p=128),
                  writes=[("wsl2", wi)])
            for j in range(2):
                oc = sl_ * 2 + j
                for ci in range(5):
                    cs, n = CHUNKS[ci]
                    b0, b1, b2, b3 = k.rot("g0", 2), 2 + k.rot("g1", 2), 4 + k.rot("g2", 2), 6 + k.rot("g3", 2)
                    for kc in range(8):
                        k.mm(k.ps[:, b0, 0:n], k.wslab[wi][:, kc, j * 128:(j + 1) * 128], hT[:, kc, cs:cs + n], kc == 0, kc == 7,
                             reads=[("wslab", wi), ("hT", kc, ci)], writes=[("ps", b0)])
                    for kc in range(2):
                        k.mm(k.ps[:, b1, 0:n], wap[:, kc, oc * 128:(oc + 1) * 128], oT[:, kc, cs:cs + n], kc == 0, kc == 1,
                             reads=["wap", ("oT", kc, ci)], writes=[("ps", b1)])
                    for kc in range(8):
                        k.mm(k.ps[:, b2, 0:n], wsl2[wi][:, kc, j * 128:(j + 1) * 128], hT[:, kc, cs:cs + n], kc == 0, kc == 7,
                             reads=[("wsl2", wi), ("hT", kc, ci)], writes=[("ps", b2)])
                    for kc in range(4):
                        k.mm(k.ps[:, b3, 0:n], wcp[:, kc, oc * 128:(oc + 1) * 128], cT[:, kc, cs:cs + n], kc == 0, kc == 3,
                             reads=["wcp", ("cT", kc, ci)], writes=[("ps", b3)])
                    i = k.rot("sga", 2)
                    k.act(sga[i][:, 0:n], k.ps[:, b0, 0:n], AF.Sigmoid, reads=[("ps", b0)], writes=[("sga", i)])
                    k.act(sgb2[i][:, 0:n], k.ps[:, b2, 0:n], AF.Sigmoid, reads=[("ps", b2)], writes=[("sgb2", i)])
                    k.tt("dve", m1[i][:, 0:n], k.ps[:, b1, 0:n], sga[i][:, 0:n], ALU.mult, reads=[("ps", b1), ("sga", i)],
                         writes=[("m1", i)])
                    k.tt("dve", m2[i][:, 0:n], k.ps[:, b3, 0:n], sgb2[i][:, 0:n], ALU.mult, reads=[("ps", b3), ("sgb2", i)],
                         writes=[("m2", i)])
                    k.tt("dve", merged[:, oc, cs:cs + n], m1[i][:, 0:n], m2[i][:, 0:n], ALU.add,
                         reads=[("m1", i), ("m2", i)], writes=[("mg", oc, ci)])

        def evac_o(oc, ci, cs, n, b):
            k.tt("dve", k.xT[:, oc, cs:cs + n], k.ps[:, b, 0:n], k.xT[:, oc, cs:cs + n], ALU.add,
                 reads=[("ps", b), ("xT", oc, ci)], writes=[("xT", oc, ci)])
        k.projB(W["w_o"][0], 0, D, merged, "mg", 8, range(5), evac_o, "wq")

    def projB(self, w2d, col0, ncols, rhs, rhs_key, nk, cis, evac, wname, banks=(0, 1, 2, 3)):
        k = self
        for s0 in range(0, ncols, 256):
            wc = min(256, ncols - s0)
            wi = k.rot(wname, 2)
            wt = k.wslab[wi]
            k.dma("pool", wt[:, 0:nk, 0:wc],
                  w2d[:, col0 + s0:col0 + s0 + wc].rearrange("(kc p) f -> p kc f", p=128),
                  writes=[("wslab", wi)])
            for j in range(wc // 128):
                oc = s0 // 128 + j
                for ci in cis:
                    cs, n = CHUNKS[ci]
                    b = banks[k.rot("pb%s" % (banks,), len(banks))]
                    for kc in range(nk):
                        k.mm(k.ps[:, b, 0:n], wt[:, kc, j * 128:(j + 1) * 128], rhs[:, kc, cs:cs + n],
                             kc == 0, kc == nk - 1, reads=[("wslab", wi), (rhs_key, kc, ci)], writes=[("ps", b)])
                    evac(oc, ci, cs, n, b)

    def xattn(self):
        k = self
        W = k.W
        P = k.P
        ar = k.ar
        ar.reset()
        mk_tok = [ar.alloc([128, 2, D], BF16) for _ in range(2)]
        mv_tok = [ar.alloc([128, 2, D], BF16) for _ in range(2)]
        mkT = [ar.alloc([128, 8, 256], BF16) for _ in range(2)]
        hT = ar.alloc([128, 8, NT], BF16)
        mark = ar.off
        sq = [ar.alloc([128, 8, 512], BF16)]
        sq2 = [ar.alloc([128, 8, 512], BF16)]
        rstd = None
        k.norm(NORM_IDX["xattn_norm"], hT, sq2, rstd)
        wkv = [ar.alloc([128, 8, 512], BF16) for _ in range(2)]
        memT = ar.alloc([128, 8, 256], F32)
        mhT = ar.alloc([128, 8, 256], BF16)
        mstg = [ar.alloc([128, D], F32) for _ in range(2)]
        ostg = [ar.alloc([128, 512], F32) for _ in range(2)]

        import os
        xsub = 9
        if xsub < -2:
            return
        for mt in range(2):
            k.dma("sp", mstg[mt][:], k.io["mem"][mt * 128:(mt + 1) * 128, :], writes=[("mstg", mt)])
            bp = 6 if mt == 0 else 4
            for kc in range(8):
                k.tr(k.ps[:, bp + kc // 4, (kc % 4) * 128:(kc % 4 + 1) * 128], mstg[mt][:, kc * 128:(kc + 1) * 128],
                     k.identf[:], reads=[("mstg", mt), "identf"], writes=[("ps", bp + kc // 4)])
            k.copy("act", memT[:, :, mt * 128:(mt + 1) * 128],
                   k.ps[:, bp:bp + 2, :].rearrange("p b (k r) -> p (b k) r", k=4),
                   reads=[("ps", bp), ("ps", bp + 1)], writes=[("memT", kc, 0) for kc in range(8)])
        if xsub < -1:
            return
        k.norm(NORM_IDX["mem_norm"], mhT, sq, rstd, chunks=[(0, 256)], src=memT, keyp="mhT", srckey="memT")
        if xsub < 0:
            return
        for cc in range(4):
            wi = k.rot("wkv", 2)
            k.dma("pool", wkv[wi][:], W["w_xkv"][0, :, cc * 512:(cc + 1) * 512].rearrange("(kc p) f -> p kc f", p=128),
                  writes=[("wkv", wi)])
            for mt in range(2):
                b = k.rot("pbkv", 2)
                for kc in range(8):
                    k.mm(k.ps[:, b, :], mhT[:, kc, mt * 128:(mt + 1) * 128], wkv[wi][:, kc, :], kc == 0, kc == 7,
                         reads=[("wkv", wi), ("mhT", kc, 0)], writes=[("ps", b)])
                xv = ""
                oi = k.rot("ostg", 2)
                if "a" not in xv:
                    k.copy("act", ostg[oi][:], k.ps[:, b, :], reads=[("ps", b)], writes=[("ostg", oi)])
                dst = (k.io["pmk"] if cc < 2 else k.io["pmv"])[mt * 128:(mt + 1) * 128, (cc % 2) * 512:(cc % 2 + 1) * 512]
                if "d" not in xv:
                    k.dma("sp", dst, ostg[oi][:], reads=[("ostg", oi)])
                tgt = (mk_tok if cc < 2 else mv_tok)[0]
                if "v" not in xv:
                    k.copy("dve", tgt[:, mt, (cc % 2) * 512:(cc % 2 + 1) * 512], ostg[oi][:], reads=[("ostg", oi)],
                           writes=[("mk_tok" if cc < 2 else "mv_tok", 0, mt, cc % 2)])

        def make_mkT(slot, mkt, keyname):
            for mt in range(2):
                b = 6 + k.rot("pbt", 2)
                psb = k.ps[:, b, :].bitcast(BF16)
                for c8 in range(8):
                    k.tr(psb[:, c8 * 128:(c8 + 1) * 128], mkt[:, mt, c8 * 128:(c8 + 1) * 128], k.ident[:],
                         reads=[(keyname, slot, mt, c8 // 4), "ident"], writes=[("ps", b)])
                k.copy("act", mkT[slot][:, :, mt * 128:(mt + 1) * 128], psb.rearrange("p (c m) -> p c m", c=8),
                       reads=[("ps", b)], writes=[("mkT", slot, mt)])
        import os
        xsub = 9
        if xsub < 1:
            return
        make_mkT(0, mk_tok[0], "mk_tok")
        if xsub < 2:
            return
        P.barrier()
        ar.reset(mark)
        QT = ar.alloc([128, 8, NT], BF16)
        k.wslab = [ar.alloc([128, 8, 256], BF16) for _ in range(2)]
        PT = [ar.alloc([128, 512], BF16) for _ in range(4)]
        rz = [ar.alloc([128, 512], F32) for _ in range(2)]

        def evac_q(oc, ci, cs, n, b):
            k.act(QT[:, oc, cs:cs + n], k.ps[:, b, 0:n], AF.Copy, reads=[("ps", b)], writes=[("QT", oc, ci)],
                  scale=1.0 / 16.0)
        k.projB(W["w_xq"][0], 0, D, hT, "hT", 8, range(5), evac_q, "wq")

        if xsub < 3:
            return
        oT = hT

        def attend(slot, cs, n, ci, qkeys_ci):
            for h in range(4):
                it = k.rot("att_it", 2)
                sb0 = 2
                ob = (0, 1) if it == 0 else (4, 5)
                zb = 6 + it
                pts = []
                for mt in range(2):
                    b = sb0 + mt
                    for dc in range(2):
                        k.mm(k.ps[:, b, 0:n], mkT[slot][:, 2 * h + dc, mt * 128:(mt + 1) * 128],
                             QT[:, 2 * h + dc, cs:cs + n], dc == 0, dc == 1,
                             reads=[("mkT", slot, mt), ("QT", 2 * h + dc, qkeys_ci)], writes=[("ps", b)])
                    pi = k.rot("PT", 4)
                    k.act(PT[pi][:, 0:n], k.ps[:, b, 0:n], AF.Exp, reads=[("ps", b)], writes=[("PT", pi)])
                    pts.append(pi)
                for dc in range(2):
                    for mt in range(2):
                        k.mm(k.ps[:, ob[dc], 0:n], mv_tok[slot][:, mt, (2 * h + dc) * 128:(2 * h + dc + 1) * 128],
                             PT[pts[mt]][:, 0:n], mt == 0, mt == 1,
                             reads=[("mv_tok", slot, mt, (2 * h + dc) // 4), ("PT", pts[mt])], writes=[("ps", ob[dc])])
                for mt in range(2):
                    k.mm(k.ps[:, zb, 0:n], k.ones[:], PT[pts[mt]][:, 0:n], mt == 0, mt == 1,
                         reads=["ones", ("PT", pts[mt])], writes=[("ps", zb)])
                ri = k.rot("rz", 2)
                P.op("dve", lambda e, ri=ri, zb=zb: e.reciprocal(out=rz[ri][:, 0:n], in_=k.ps[:, zb, 0:n]),
                     reads=[("ps", zb)], writes=[("rz", ri)])
                for dc in range(2):
                    k.tt("dve", oT[:, 2 * h + dc, cs:cs + n], k.ps[:, ob[dc], 0:n], rz[ri][:, 0:n], ALU.mult,
                         reads=[("ps", ob[dc]), ("rz", ri)], writes=[("hT", 2 * h + dc, ci)])

        for ci in range(4):
            cs, n = CHUNKS[ci]
            attend(0, cs, n, ci, ci)
        if xsub < 4:
            return
        for bb in range(4):
            slot = 1
            for mt in range(2):
                k.dma("pool", mk_tok[1][:, mt, :], k.io["cmk"][bb, mt * 128:(mt + 1) * 128, :],
                      writes=[("mk_tok", 1, mt, 0), ("mk_tok", 1, mt, 1)])
                k.dma("pool", mv_tok[1][:, mt, :], k.io["cmv"][bb, mt * 128:(mt + 1) * 128, :],
                      writes=[("mv_tok", 1, mt, 0), ("mv_tok", 1, mt, 1)])
            make_mkT(1, mk_tok[1], "mk_tok")
            attend(1, S + bb * 8, 8, 4, 4)

        if xsub < 5:
            return
        def evac_o(oc, ci, cs, n, b):
            k.tt("dve", k.xT[:, oc, cs:cs + n], k.ps[:, b, 0:n], k.xT[:, oc, cs:cs + n], ALU.add,
                 reads=[("ps", b), ("xT", oc, ci)], writes=[("xT", oc, ci)])
        k.projB(W["w_xo"][0], 0, D, oT, "hT", 8, range(5), evac_o, "wq")

    def final_norm(self):
        k = self
        ar = k.ar
        ar.reset()
        gbc = ar.alloc([128, D], F32)
        junk = [ar.alloc([128, D], BF16) for _ in range(2)]
        ost = [ar.alloc([128, D], F32) for _ in range(3)]
        stat = ar.alloc([128, 17, 2], F32)
        k.dma("sp", gbc[:], k.W["final_norm"].partition_broadcast(128), writes=["gbc"])
        for t in range(17):
            R = 128 if t < 16 else 32
            bp = 6 if t % 2 == 0 else 4
            ci = min(t // 4, 4)
            for kc in range(8):
                k.tr(k.ps[0:R, bp + kc // 4, (kc % 4) * 128:(kc % 4 + 1) * 128],
                     k.xT[:, kc, t * 128:t * 128 + R], k.identf[:, :],
                     reads=[("xT", kc, ci), "identf"], writes=[("ps", bp + kc // 4)])
            xt = k.ps[0:R, bp:bp + 2, :].rearrange("p b c -> p (b c)")
            ji = k.rot("fjunk", 2)
            k.act(junk[ji][0:R, :], xt, AF.Square, reads=[("ps", bp), ("ps", bp + 1)], writes=[("fstat", t, 0)],
                  accum_out=stat[0:R, t, 0:1])
            k.ts("dve", stat[0:R, t, 1:2], stat[0:R, t, 0:1], 1.0 / D, EPS, ALU.mult, ALU.add,
                 reads=[("fstat", t, 0)], writes=[("fstat", t, 1)])
            k.tt("pool", stat[0:R, t, 1:2], stat[0:R, t, 1:2], k.mhalf[0:R, 0:1], ALU.pow,
                 reads=[("fstat", t, 1), "mhalf"], writes=[("fstat", t, 1)])
            oi = k.rot("ost", 3)
            k.stt(ost[oi][0:R, :], xt, stat[0:R, t, 1:2], gbc[0:R, :], ALU.mult, ALU.mult,
                  reads=[("ps", bp), ("ps", bp + 1), ("fstat", t, 1), "gbc"], writes=[("ost", oi)])
            dst = k.io["yp"][t * 128:(t + 1) * 128, :] if t < 16 else k.io["ys"][:, :]
            k.dma("sp", dst, ost[oi][0:R, :], reads=[("ost", oi)])

    def norm_chunk(self, gi, yv, sq, rstd, ci, cs, n, yi):
        k = self
        rb, rkey = k.rstd_bc(k.xT, "xT", ci, cs, n, sq, banks=(2, 3))
        for kc in range(8):
            k.stt(yv[:, kc, 0:n], k.xT[:, kc, cs:cs + n], k.gcol[:, gi, kc:kc + 1], rb,
                  ALU.mult, ALU.mult, reads=[("xT", kc, ci), rkey, ("gcol", gi)], writes=[("yT", yi)])


_CACHE = {}


def _get_nc(stage):
    if stage not in _CACHE:
        b = Builder(stage)
        _CACHE[stage] = b.build()
    return _CACHE[stage]


def _rel_bucket_np(n):
    n = np.maximum(n, 0)
    nf = np.maximum(n, 1).astype(np.float32)
    large = 16 + ((np.log(nf / np.float32(16)) / np.float32(np.log(128.0))) * np.float32(16)).astype(np.int32)
    return np.where(n < 16, n, np.minimum(large, 31))


def _ebuck():
    E = np.zeros((3, 33, 384), np.float32)
    for g, (w, d) in enumerate(GROUPS):
        for u in range(383):
            delta = u - 127
            if 0 <= delta <= 128:
                E[g, int(_rel_bucket_np(np.array(delta * d))), u] = 1.0
            else:
                E[g, 32, u] = NEG
        E[g, 32, 383] = NEG
    return E


def kernel(_stage="full", **inputs):
    nc = _get_nc(_stage)
    eb = _ebuck()
    _cache_names = ("cache_win0_k", "cache_win0_v", "cache_win1_k", "cache_win1_v", "cache_win2_k", "cache_win2_v")
    assert all(n in inputs for n in _cache_names)
    f32 = lambda a: np.ascontiguousarray(np.asarray(a, dtype=np.float32))
    in_maps = []
    for c in range(NCORES):
        m = {"xp": f32(inputs["x_prompt"][c]),
             "xs": f32(inputs["x_sample"][4 * c:4 * c + 4]).reshape(NS_TOK, D),
             "mem": f32(inputs["mem_prompt"][c]),
             "stc": f32(inputs["state_conv"][0, 4 * c:4 * c + 4]),
             "cmk": f32(inputs["cache_mem_k"][0, 4 * c:4 * c + 4]).reshape(4, 256, D),
             "cmv": f32(inputs["cache_mem_v"][0, 4 * c:4 * c + 4]).reshape(4, 256, D)}
        for g in range(3):
            m["cw%dk" % g] = f32(inputs["cache_win%d_k" % g][0, 4 * c:4 * c + 4]).reshape(4, GROUPS[g][0], 256)
            m["cw%dv" % g] = f32(inputs["cache_win%d_v" % g][0, 4 * c:4 * c + 4]).reshape(4, GROUPS[g][0], 256)
        for n in WNAMES:
            m[n] = f32(inputs[n])
        m["ebuck"] = eb
        in_maps.append(m)
    res = run_bass_kernel_spmd(nc, in_maps, core_ids=list(range(NCORES)))
    R = res.results
    cat = lambda name: np.stack([np.asarray(R[c][name]) for c in range(NCORES)], axis=0)
    y_prompt = cat("yp")
    y_sample = cat("ys").reshape(32, 8, D)
    outs = [y_prompt, y_sample]
    for g in range(3):
        Lg = GROUPS[g][0]
        outs.append(cat("pw%dk" % g).reshape(1, 8, Lg, 4, 64))
        outs.append(cat("pw%dv" % g).reshape(1, 8, Lg, 4, 64))
    outs.append(cat("pconv").reshape(1, 8, 30, 512))
    outs.append(cat("pmk").reshape(1, 8, 256, 4, 256))
    outs.append(cat("pmv").reshape(1, 8, 256, 4, 256))
    for g in range(3):
        Lg = GROUPS[g][0]
        outs.append(cat("sw%dk" % g).reshape(1, 32, Lg, 4, 64))
        outs.append(cat("sw%dv" % g).reshape(1, 32, Lg, 4, 64))
    outs.append(cat("sconv").reshape(1, 32, 30, 512))
    return tuple(outs)
```
